# Optimizing a Trainium2 kernel written in Bass

```python
import math
import jax, jax.numpy as jnp
from jax import lax
import numpy as np

D_MODEL = 1024
BATCH = 8
SEQ = 2048
DEPTH = 1

CHUNK = 64
Q_BLOCK = 128
HEAD_DIM = 64
N_DIFF_HEADS = 4
DIFF_V_DIM = 2 * HEAD_DIM
N_SB_HEADS = 8
DIFF_WIDTH = N_DIFF_HEADS * DIFF_V_DIM
SB_WIDTH = N_SB_HEADS * HEAD_DIM
MIX_WIDTH = DIFF_WIDTH + SB_WIDTH
DIFF_QK_WIDTH = N_DIFF_HEADS * 2 * HEAD_DIM
IN_SPLITS = (DIFF_QK_WIDTH, 2 * DIFF_QK_WIDTH, 2 * DIFF_QK_WIDTH + DIFF_WIDTH,
             2 * DIFF_QK_WIDTH + DIFF_WIDTH + SB_WIDTH,
             2 * DIFF_QK_WIDTH + DIFF_WIDTH + 2 * SB_WIDTH)
IN_WIDTH = 2 * DIFF_QK_WIDTH + DIFF_WIDTH + 3 * SB_WIDTH
D_FF = 4 * D_MODEL
EPS = 1e-6
NEG_INF = -1e30

kernel_name = "hymba_diffattn_stickbreak_sqrelu_block"


def rms_norm(x, g):
    xf = x.astype(jnp.float32)
    y = xf * lax.rsqrt(jnp.mean(xf * xf, axis=-1, keepdims=True) + EPS)
    return (y * g.astype(jnp.float32)).astype(x.dtype)


def alibi_slopes(n_heads):
    return jnp.exp2(-8.0 * jnp.arange(1, n_heads + 1, dtype=jnp.float32) / n_heads)


def diff_attention(q, k, v, lam, lambda_init, subln_g):
    B, S, H = q.shape[0], q.shape[1], q.shape[2]
    scale = 1.0 / math.sqrt(HEAD_DIM)
    slopes = alibi_slopes(H)
    outs = []
    for i in range(S // Q_BLOCK):
        t0, t1 = i * Q_BLOCK, (i + 1) * Q_BLOCK
        qb = q[:, t0:t1].astype(jnp.float32)
        kb = k[:, :t1].astype(jnp.float32)
        scores = jnp.einsum('bqhmd,bkhmd->bhmqk', qb, kb) * scale
        tpos = jnp.arange(t0, t1)
        spos = jnp.arange(t1)
        mask = (spos // CHUNK)[None, :] <= (tpos // CHUNK)[:, None]
        dist = jnp.abs(tpos[:, None] - spos[None, :]).astype(jnp.float32)
        bias = -slopes[:, None, None] * dist
        scores = jnp.where(mask[None, None, None], scores + bias[None, :, None], NEG_INF)
        p = jax.nn.softmax(scores, axis=-1)
        attn = p[:, :, 0] - lam * p[:, :, 1]
        outs.append(jnp.einsum('bhqk,bkhe->bqhe', attn, v[:, :t1].astype(jnp.float32)))
    o = jnp.concatenate(outs, axis=1)
    o = rms_norm(o, subln_g) * (1.0 - lambda_init)
    return o.reshape(B, S, H * DIFF_V_DIM).astype(q.dtype)


def stick_breaking(q, k, v, norm_g):
    B, S, H = q.shape[0], q.shape[1], q.shape[2]
    scale = 1.0 / math.sqrt(HEAD_DIM)
    outs = []
    for i in range(S // Q_BLOCK):
        t0, t1 = i * Q_BLOCK, (i + 1) * Q_BLOCK
        qb = q[:, t0:t1].astype(jnp.float32)
        kb = k[:, :t1].astype(jnp.float32)
        z = jnp.einsum('bqhd,bkhd->bhqk', qb, kb) * scale
        tpos = jnp.arange(t0, t1)
        spos = jnp.arange(t1)
        mask = (spos[None, :] < tpos[:, None])[None, None]
        neg_log_1m_beta = jnp.where(mask, jax.nn.softplus(z), 0.0)
        suffix = lax.cumsum(neg_log_1m_beta, axis=3, reverse=True)
        log_a = jax.nn.log_sigmoid(z) - (suffix - neg_log_1m_beta)
        a = jnp.where(mask, jnp.exp(log_a), 0.0)
        outs.append(jnp.einsum('bhqk,bkhd->bqhd', a, v[:, :t1].astype(jnp.float32)))
    o = jnp.concatenate(outs, axis=1)
    o = rms_norm(o, norm_g)
    return o.reshape(B, S, H * HEAD_DIM).astype(q.dtype)


def setup_inputs(seed: int = 0) -> dict:
    key = jax.random.key(seed)
    ks = jax.random.split(key, 16)
    f32 = jnp.float32
    def nrm(k, shape, scale):
        return jax.random.normal(k, shape, f32) * scale
    return {
        "x": nrm(ks[0], (BATCH, SEQ, D_MODEL), 1.0),
        "norm1_g": 1.0 + nrm(ks[1], (DEPTH, D_MODEL), 0.02),
        "w_in": nrm(ks[2], (DEPTH, D_MODEL, IN_WIDTH), D_MODEL ** -0.5),
        "lambda_q1": nrm(ks[3], (DEPTH, HEAD_DIM), 0.1),
        "lambda_k1": nrm(ks[4], (DEPTH, HEAD_DIM), 0.1),
        "lambda_q2": nrm(ks[5], (DEPTH, HEAD_DIM), 0.1),
        "lambda_k2": nrm(ks[6], (DEPTH, HEAD_DIM), 0.1),
        "diff_subln_g": 1.0 + nrm(ks[7], (DEPTH, DIFF_V_DIM), 0.02),
        "sb_norm_g": 1.0 + nrm(ks[8], (DEPTH, HEAD_DIM), 0.02),
        "w_out": nrm(ks[9], (DEPTH, MIX_WIDTH, D_MODEL), MIX_WIDTH ** -0.5),
        "norm2_g": 1.0 + nrm(ks[10], (DEPTH, D_MODEL), 0.02),
        "w_up": nrm(ks[11], (DEPTH, D_MODEL, D_FF), D_MODEL ** -0.5),
        "w_down": nrm(ks[12], (DEPTH, D_FF, D_MODEL), D_FF ** -0.5),
        "final_norm_g": 1.0 + nrm(ks[13], (D_MODEL,), 0.02),
    }


def reference(x, norm1_g, w_in, lambda_q1, lambda_k1, lambda_q2, lambda_k2,
              diff_subln_g, sb_norm_g, w_out, norm2_g, w_up, w_down, final_norm_g):
    B, S = x.shape[0], x.shape[1]
    h = x
    for l in range(DEPTH):
        lambda_init = 0.8 - 0.6 * math.exp(-0.3 * l)
        n = rms_norm(h, norm1_g[l])
        proj = jnp.einsum('bsd,de->bse', n, w_in[l])
        dq, dk, dv, sq, sk, sv = jnp.split(proj, IN_SPLITS, axis=-1)
        dq = dq.reshape(B, S, N_DIFF_HEADS, 2, HEAD_DIM)
        dk = dk.reshape(B, S, N_DIFF_HEADS, 2, HEAD_DIM)
        dv = dv.reshape(B, S, N_DIFF_HEADS, DIFF_V_DIM)
        sq = sq.reshape(B, S, N_SB_HEADS, HEAD_DIM)
        sk = sk.reshape(B, S, N_SB_HEADS, HEAD_DIM)
        sv = sv.reshape(B, S, N_SB_HEADS, HEAD_DIM)
        lam = (jnp.exp(jnp.sum(lambda_q1[l].astype(jnp.float32) * lambda_k1[l].astype(jnp.float32)))
               - jnp.exp(jnp.sum(lambda_q2[l].astype(jnp.float32) * lambda_k2[l].astype(jnp.float32)))
               + lambda_init)
        o_diff = diff_attention(dq, dk, dv, lam, lambda_init, diff_subln_g[l])
        o_sb = stick_breaking(sq, sk, sv, sb_norm_g[l])
        mixed = jnp.concatenate([o_diff, o_sb], axis=-1)
        h = h + jnp.einsum('bse,ed->bsd', mixed, w_out[l])
        n2 = rms_norm(h, norm2_g[l])
        u = jnp.einsum('bsd,df->bsf', n2, w_up[l])
        u = jnp.square(jax.nn.relu(u))
        h = h + jnp.einsum('bsf,fd->bsd', u, w_down[l])
    return rms_norm(h, final_norm_g)
```

```python
import os
from contextlib import ExitStack
from itertools import zip_longest

import numpy as np
import concourse.bass as bass
import concourse.mybir as mybir
from concourse.bass_utils import run_bass_kernel_spmd

F32 = mybir.dt.float32
BF16 = mybir.dt.bfloat16
AF = mybir.ActivationFunctionType
ALU = mybir.AluOpType
AX = mybir.AxisListType

D = 1024
S = 2048
NT = 4
TW = 512
NB = 16
EPS = 1e-6
NEG = -30000.0
SLOPES = [2.0 ** (-8.0 * (h + 1) / 4.0) for h in range(4)]
LAMBDA_INIT = 0.8 - 0.6

ENGINES = ("tensor", "scalar", "vector", "gpsimd", "sync")
SAME_ENGINE_SYNC = True
DUMMY_SB = int(os.environ.get('MK_DUMMY_SB', '0'))
DUMMY_DIFF = int(os.environ.get('MK_DUMMY_DIFF', '0'))


class Buf:
    __slots__ = ("name", "w", "r")

    def __init__(self, name=""):
        self.name = name
        self.w = None
        self.r = []


class Op:
    __slots__ = ("eng", "fn", "deps", "signal", "sigval", "dma_slot", "dma_val", "hoist", "seq")

    def __init__(self, eng, fn):
        self.eng = eng
        self.fn = fn
        self.deps = []
        self.signal = False
        self.sigval = None
        self.dma_slot = None
        self.dma_val = None
        self.hoist = False
        self.seq = -1


class DmaSlot:
    def __init__(self, sem):
        self.sem = sem
        self.count = 0


class Sched:
    def __init__(self, nc, es):
        self.nc = nc
        self.es = es
        self.ops = {e: [] for e in ENGINES}
        self.sems = {e: es.enter_context(nc.semaphore("s_" + e)) for e in ENGINES}
        self.slots = []

    def new_slot(self, name):
        s = DmaSlot(self.es.enter_context(self.nc.semaphore("d_" + name)))
        self.slots.append(s)
        return s

    def _add_dep(self, op, dep):
        if dep is None or dep is op:
            return
        if dep.fn is None and dep.dma_slot is None:
            return
        if dep.dma_slot is None:
            if dep.eng == op.eng:
                if op.eng == "tensor" or not SAME_ENGINE_SYNC:
                    return
            dep.signal = True
        op.deps.append(dep)

    def op(self, eng, fn, r=(), w=()):
        o = Op(eng, fn)
        self.seqc = getattr(self, "seqc", 0) + 1
        o.seq = self.seqc
        for b in r:
            self._add_dep(o, b.w)
        for b in w:
            self._add_dep(o, b.w)
            for rd in b.r:
                self._add_dep(o, rd)
        for b in w:
            b.w = o
            b.r = []
        for b in r:
            if b not in w:
                b.r.append(o)
        self.ops[eng].append(o)
        return o

    def group_begin(self, eng):
        self._gstart = (eng, len(self.ops[eng]))
        self._gseq = getattr(self, "seqc", 0)

    def group_end(self, eng):
        e, start = self._gstart
        assert e == eng
        grp = self.ops[eng][start:]
        if len(grp) < 2:
            return
        o = Op(eng, None)
        o.hoist = True
        for g in grp:
            o.deps.extend(d for d in g.deps if 0 <= d.seq <= self._gseq)
        self.ops[eng].insert(start, o)

    def fence(self, engs, bufs):
        for e in engs:
            o = Op(e, None)
            for b in bufs:
                self._add_dep(o, b.w)
                for rd in b.r:
                    self._add_dep(o, rd)
            self.ops[e].append(o)

    def dma(self, eng, out, in_, slot, r=(), w=()):
        o = self.op(eng, lambda e: e.dma_start(out=out, in_=in_), r=r, w=w)
        slot.count += 16
        o.dma_slot = slot
        o.dma_val = slot.count
        return o

    def barrier(self):
        lasts = []
        for e in ENGINES:
            real = [o for o in self.ops[e] if o.fn is not None]
            if real:
                lasts.append(real[-1])
        for e in ENGINES:
            o = Op(e, None)
            for l in lasts:
                if l.eng != e:
                    if l.dma_slot is None:
                        l.signal = True
                    o.deps.append(l)
            for sl in self.slots:
                if sl.count:
                    d = Op("dma", None)
                    d.dma_slot = sl
                    d.dma_val = sl.count
                    o.deps.append(d)
            self.ops[e].append(o)

    def emit(self, block):
        for e in ENGINES:
            c = 0
            for o in self.ops[e]:
                if o.signal and o.fn is not None:
                    c += 1
                    o.sigval = c
        sems = self.sems

        def run(engname):
            def body(eng):
                waited = {}
                for o in self.ops[engname]:
                    need = {}
                    for d in o.deps:
                        if d.dma_slot is not None:
                            sem, val = d.dma_slot.sem, d.dma_val
                        else:
                            sem, val = sems[d.eng], d.sigval
                        key = id(sem)
                        if key not in need or need[key][1] < val:
                            need[key] = (sem, val)
                    for key, (sem, val) in need.items():
                        if waited.get(key, 0) < val:
                            eng.wait_ge(sem, val)
                            waited[key] = val
                    if o.fn is None:
                        continue
                    ins = o.fn(eng)
                    if o.dma_slot is not None:
                        ins.then_inc(o.dma_slot.sem, 16)
                    elif o.signal:
                        ins.then_inc(sems[engname], 1)
            return body

        block.tensor(run("tensor"))
        block.scalar(run("scalar"))
        block.vector(run("vector"))
        block.gpsimd(run("gpsimd"))
        block.sync(run("sync"))


class Arena:
    def __init__(self, ap, words):
        self.ap = ap
        self.words = words
        self.off = 0

    def reset(self):
        self.off = 0

    def alloc(self, shape, dtype):
        n = int(np.prod(shape))
        words = n if dtype == F32 else (n + 1) // 2
        assert self.off + words <= self.words, (self.off, words, self.words)
        a = self.ap[:, self.off:self.off + words]
        self.off += words
        if dtype != F32:
            a = a.bitcast(dtype)
        if len(shape) == 2:
            a = a.rearrange("p (a b) -> p a b", a=shape[0])
        elif len(shape) == 3:
            a = a.rearrange("p (a b c) -> p a b c", a=shape[0], b=shape[1])
        return a


class Rot:
    def __init__(self, items):
        self.items = list(items)
        self.i = 0

    def next(self):
        x = self.items[self.i % len(self.items)]
        self.i += 1
        return x


NCT = 7 * 128 + 16 * 128
ARENA_WORDS = 116 * 256


def build_program(debug=False, stop=None):
    nc = bass.Bass("TRN2", target_bir_lowering=False)
    dt = nc.dram_tensor
    xT_d = dt("xT", [D, S], F32, kind="ExternalInput").ap()
    win_d = dt("win", [8, 128, 8 * 384], F32, kind="ExternalInput").ap()
    wout_d = dt("wout", [128, 8 * 1024], F32, kind="ExternalInput").ap()
    wup_d = dt("wup", [8, 128, 8 * 512], F32, kind="ExternalInput").ap()
    wdn_d = dt("wdn", [8, 128, 4 * 1024], F32, kind="ExternalInput").ap()
    gv_d = dt("gv", [128, 26], F32, kind="ExternalInput").ap()
    lam_d = dt("lamv", [128, 256], F32, kind="ExternalInput").ap()
    ctab_d = dt("ctab", [128, NCT], F32, kind="ExternalInput").ap()
    nu_d = dt("nutab", [128, NB * 128], F32, kind="ExternalInput").ap()
    qaug_d = dt("qaug", [4, S], F32, kind="ExternalInput").ap()
    kaug_d = dt("kaug", [4, 4, S], F32, kind="ExternalInput").ap()
    outT_d = dt("outT", [D, S], F32, kind="ExternalOutput").ap()
    dbg_d = None
    if debug:
        dbg_d = dt("dbg", [D, S], F32, kind="ExternalOutput").ap()

    xTv = xT_d.rearrange("(c p) t -> p c t", p=128)
    outTv = outT_d.rearrange("(c p) t -> p c t", p=128)

    with ExitStack() as es:
        sb = lambda name, shape, dtype: es.enter_context(nc.sbuf_tensor(name, shape, dtype))
        nT = sb("nT", [128, 8, S], BF16)
        mixT = sb("mixT", [128, 8, S], BF16)
        sq = sb("sq", [128, 8, TW], BF16)
        rs_t = [sb(f"rs{i}", [128, TW], F32) for i in range(2)]
        ln_t = sb("lnv", [128, TW], F32)
        ctab = sb("ctab_sb", [128, NCT], BF16)
        ones = sb("ones", [128, 128], BF16)
        bones = sb("bones", [128, 128], BF16)
        nu = sb("nu", [128, NB * 128], BF16)
        gv = sb("gv_sb", [128, 26], F32)
        lamt = sb("lamt", [128, 256], F32)
        lamw = sb("lamw", [128, 128], F32)
        lams = sb("lams", [128, 8], F32)
        arena_t = sb("arena", [128, ARENA_WORDS], F32)
        arena = Arena(arena_t[:, :], ARENA_WORDS)
        banks = [es.enter_context(nc.psum_tensor(f"pb{i}", [128, TW], F32)) for i in range(8)]
        bankb = [Buf(f"bank{i}") for i in range(8)]

        sc = Sched(nc, es)
        ident = ctab[:, 0:128]
        negtri = ctab[:, 128:256]
        mtab = ctab[:, 256:384]
        dtabs = [ctab[:, 384 + 128 * h: 512 + 128 * h] for h in range(4)]
        esel = ctab[:, 896:896 + 16 * 128]
        g1 = lambda c: gv[:, c:c + 1]
        g2 = lambda c: gv[:, 8 + c:9 + c]
        gf = lambda c: gv[:, 16 + c:17 + c]
        gsub = lams[:, 4:5]
        gsb = gv[:, 25:26]
        neglam = lams[:, 3:4]

        b_const = Buf("const")
        b_gv = Buf("gv")
        b_lam = Buf("lam")
        s_c = [sc.new_slot(f"c{i}") for i in range(4)]
        sc.dma("gpsimd", ctab[:, :], ctab_d[:, :], s_c[0], w=[b_const])
        sc.dma("gpsimd", nu[:, :], nu_d[:, :], s_c[1], w=[b_const])
        sc.dma("sync", gv[:, :], gv_d[:, :], s_c[2], w=[b_gv])
        sc.dma("sync", lamt[:, :], lam_d[:, :], s_c[3], w=[b_lam])
        b_ones = Buf("ones")
        sc.op("gpsimd", lambda e: e.memset(ones[:, :], 1.0), w=[b_ones])
        sc.op("gpsimd", lambda e: e.memset(bones[:, :], 0.0), w=[b_ones])
        sc.op("gpsimd", lambda e: e.memset(bones[0:64, 0:64], 1.0), w=[b_ones])
        sc.op("gpsimd", lambda e: e.memset(bones[64:128, 64:128], 1.0), w=[b_ones])
        b_lw = Buf("lamw")
        sc.op("vector", lambda e: e.tensor_tensor(out=lamw[:, 0:64], in0=lamt[:, 0:64], in1=lamt[:, 64:128], op=ALU.mult), r=[b_lam], w=[b_lw])
        sc.op("vector", lambda e: e.tensor_tensor(out=lamw[:, 64:128], in0=lamt[:, 128:192], in1=lamt[:, 192:256], op=ALU.mult), r=[b_lam], w=[b_lw])
        sc.op("vector", lambda e: e.tensor_reduce(out=lams[:, 0:1], in_=lamw[:, 0:64], axis=AX.X, op=ALU.add), r=[b_lw], w=[b_lw])
        sc.op("vector", lambda e: e.tensor_reduce(out=lams[:, 1:2], in_=lamw[:, 64:128], axis=AX.X, op=ALU.add), r=[b_lw], w=[b_lw])
        sc.op("scalar", lambda e: e.activation(out=lams[:, 0:2], in_=lams[:, 0:2], func=AF.Exp), r=[b_lw], w=[b_lw])
        sc.op("vector", lambda e: e.tensor_tensor(out=lams[:, 2:3], in0=lams[:, 1:2], in1=lams[:, 0:1], op=ALU.subtract), r=[b_lw], w=[b_lw])
        sc.op("vector", lambda e: e.tensor_scalar(out=lams[:, 3:4], in0=lams[:, 2:3], scalar1=-LAMBDA_INIT, scalar2=None, op0=ALU.add), r=[b_lw], w=[b_lw])
        sc.op("vector", lambda e: e.tensor_scalar(out=lams[:, 4:5], in0=gv[:, 24:25], scalar1=1.0 - LAMBDA_INIT, scalar2=None, op0=ALU.mult), r=[b_gv, b_lw], w=[b_lw])

        nTb = [[Buf(f"nT{t}_{c}") for c in range(8)] for t in range(NT)]
        mixb = [[Buf(f"mix{c}_{t}") for t in range(NT)] for c in range(8)]
        sqbs = [Buf(f"sq{c}") for c in range(8)]
        rsb = [Buf("rs0"), Buf("rs1")]
        lnb = Buf("lnv")
        rsrot = Rot([0, 1])

        def rms_stats(src_c, src_bufs, nch, bank_i, inv_n, lhs_ones):
            for c in range(nch):
                sc.op("scalar", lambda e, c=c: e.activation(out=sq[:, c, :], in_=src_c(c), func=AF.Square),
                      r=src_bufs, w=[sqbs[c]])
            for c in range(nch):
                sc.op("tensor", lambda e, c=c: e.matmul(banks[bank_i][:, :], lhsT=lhs_ones, rhs=sq[:, c, :],
                                                        start=(c == 0), stop=(c == nch - 1)),
                      r=[sqbs[c], b_ones], w=[bankb[bank_i]])
            ri = rsrot.next()
            sc.op("scalar", lambda e: e.activation(out=ln_t[:, :], in_=banks[bank_i][:, :], func=AF.Ln, bias=EPS, scale=inv_n),
                  r=[bankb[bank_i]], w=[lnb])
            sc.op("scalar", lambda e: e.activation(out=rs_t[ri][:, :], in_=ln_t[:, :], func=AF.Exp, scale=-0.5),
                  r=[lnb], w=[rsb[ri]])
            return rs_t[ri], rsb[ri]

        arena.reset()
        CS_ = [arena.alloc([1, TW], BF16) for _ in range(2)]
        rl = arena.alloc([1, TW], F32)[:, 0, :]
        rlb = Buf("rl")
        wg1 = arena.alloc([8, 384], BF16)
        wg = [wg1, wg1]
        wgb1 = Buf("wg")
        wgb = [wgb1, wgb1]
        wgs1 = sc.new_slot("wg")
        wgslot = [wgs1, wgs1]
        V = [arena.alloc([NB, 128], BF16) for _ in range(2)]
        QA = [arena.alloc([1, S], BF16) for _ in range(2)]
        KA = [arena.alloc([1, S], BF16) for _ in range(2)]
        QB = [arena.alloc([1, S], BF16) for _ in range(2)]
        u_off = arena.off
        KB = [arena.alloc([1, S], BF16) for _ in range(2)]
        P_ = [arena.alloc([1, TW], BF16) for _ in range(4)]
        arena.alloc([4, TW], BF16)
        EB = [Arena(arena.ap[:, u_off:u_off + 4096], 4096).alloc([NB, TW], BF16), arena.alloc([NB, TW], BF16)]
        EBb = [[Buf(f"EB{i}_{j}") for j in range(NB)] for i in range(2)]
        QA = [a[:, 0, :] for a in QA]; QB = [a[:, 0, :] for a in QB]
        KA = [a[:, 0, :] for a in KA]; KB = [a[:, 0, :] for a in KB]
        CS = [a[:, 0, :] for a in CS_]
        CSb = [Buf("CS0"), Buf("CS1")]
        qb_ = [[Buf(f"q{s}_{t}") for t in range(NT)] for s in range(2)]
        kb_ = [[Buf(f"k{s}_{t}") for t in range(NT)] for s in range(2)]
        qB_ = [[Buf(f"qB{s}_{t}") for t in range(NT)] for s in range(2)]
        kB_ = [[Buf(f"kB{s}_{t}") for t in range(NT)] for s in range(2)]
        vb_ = [[Buf(f"v{s}_{t}") for t in range(NT)] for s in range(2)]
        augslot = [[sc.new_slot(f"aug{i}_{j}") for j in range(2)] for i in range(4)]
        P = [a[:, 0, :] for a in P_]
        Pb = [Buf(f"P{i}") for i in range(4)]
        SP = [arena.alloc([NB, TW], BF16) for _ in range(2)]
        SPb = [[Buf(f"SP{i}_{j}") for j in range(NB)] for i in range(2)]
        Dt = [arena.alloc([1, TW], BF16)[:, 0, :] for _ in range(3)]
        Dtb = [Buf(f"Dt{i}") for i in range(3)]
        A = [arena.alloc([1, TW], BF16)[:, 0, :] for _ in range(5)]
        Ab = [Buf(f"A{i}") for i in range(5)]
        af = [sq[:, 2 + 2 * i:4 + 2 * i, :].rearrange("p a b -> p (a b)").bitcast(F32) for i in range(3)]
        af.append(arena.alloc([1, TW], F32)[:, 0, :])
        afb = [Buf(f"af{i}") for i in range(4)]

        for s_ in range(2):
            sc.op("gpsimd", lambda e, s_=s_: e.memset(QB[s_][0:64, :], 0.0), w=qB_[s_])
            sc.op("gpsimd", lambda e, s_=s_: e.memset(KB[s_][0:64, :], 0.0), w=kB_[s_])
            sc.op("gpsimd", lambda e, s_=s_: e.memset(QA[s_][64:128, :], 0.0), w=qb_[s_])
            sc.op("gpsimd", lambda e, s_=s_: e.memset(KA[s_][64:128, :], 0.0), w=kb_[s_])
            sc.op("gpsimd", lambda e, s_=s_: e.memset(CS[s_][:, :], 0.0), w=[CSb[s_]])
            sc.dma("gpsimd", QA[s_][64:68, :], qaug_d[:, :], augslot[0][s_], w=qb_[s_])
            sc.dma("gpsimd", QB[s_][60:64, :], qaug_d[:, :], augslot[1][s_], w=qB_[s_])

        NXS = 3
        xar = Arena(arena.ap[:, ARENA_WORDS - NXS * 4096:ARENA_WORDS], NXS * 4096)
        xs = [xar.alloc([8, TW], F32) for _ in range(NXS)]
        xsb = [Buf(f"xs{i}") for i in range(NXS)]
        xslot = [sc.new_slot(f"x{i}") for i in range(NXS)]
        for t in range(NT):
            k = t % NXS
            cols = slice(t * TW, (t + 1) * TW)
            sc.dma("sync", xs[k][:, :, :], xTv[:, :, cols], xslot[k], w=[xsb[k]])
            rs, rb = rms_stats(lambda c, k=k: xs[k][:, c, :], [xsb[k]], 8, 7, 1.0 / D, ones[:, :])
            for c in range(8):
                sc.op("vector", lambda e, c=c, k=k, rs=rs, cols=cols: e.scalar_tensor_tensor(
                    out=nT[:, c, cols], in0=xs[k][:, c, :], scalar=g1(c), in1=rs[:, :], op0=ALU.mult, op1=ALU.mult),
                    r=[xsb[k], rb, b_gv], w=[nTb[t][c]])

        def dump_and_finish(src3d, bufs):
            dslot = sc.new_slot("dbg")
            dbgv = dbg_d.rearrange("(c p) t -> p c t", p=128)
            arena.reset()
            dstage = arena.alloc([8, S], F32)
            dsb = Buf("dstage")
            for c in range(8):
                sc.op("vector", lambda e, c=c: e.tensor_copy(out=dstage[:, c, :], in_=src3d[:, c, :]), r=bufs, w=[dsb])
            sc.dma("sync", dbgv[:, :, :], dstage[:, :, :], dslot, r=[dsb])
            sc.barrier()
            block = es.enter_context(nc.Block())
            sc.emit(block)
            return nc

        if stop == "A":
            return dump_and_finish(nT, [b for l in nTb for b in l])


        wo = nT[:, 0:4, :].rearrange("p a b -> p (a b)").rearrange("p (a b) -> p a b", a=8)
        wob = Buf("wo")
        woslot = sc.new_slot("wo")
        scorerot = Rot([0, 1, 2])

        def project_chunks(g, brot_, act_ok):
            s_ = g % 2
            is_diff = g < 4
            chunks = []

            def c_dma():
                sc.dma("gpsimd", wg[s_].rearrange("p a b -> p (a b)"), win_d[g, :, :], wgslot[s_], w=[wgb[s_]])
                if is_diff:
                    sc.dma("gpsimd", KA[s_][64:68, :], kaug_d[g, :, :], augslot[2][s_], w=kb_[s_])
                    sc.dma("gpsimd", KB[s_][60:64, :], kaug_d[g, :, :], augslot[3][s_], w=kB_[s_])
                elif g in (4, 5):
                    sc.op("gpsimd", lambda e: e.memset(QA[s_][64:128, :], 0.0), w=qb_[s_])
                    sc.op("gpsimd", lambda e: e.memset(QB[s_][0:64, :], 0.0), w=qB_[s_])
            chunks.append(c_dma)

            def evac(use_act, out, in_, scale, rb, wb):
                if (use_act and act_ok) or act_ok == "all":
                    sc.op("scalar", lambda e: e.activation(out=out, in_=in_, func=AF.Copy, scale=scale), r=rb, w=wb)
                else:
                    sc.op("vector", lambda e: e.tensor_scalar(out=out, in0=in_, scalar1=scale, scalar2=None, op0=ALU.mult),
                          r=rb, w=wb)

            def mk_qk(part, t):
                def c_qk():
                    bi = brot_.next()
                    cols = slice(t * TW, (t + 1) * TW)
                    for c in range(8):
                        sc.op("tensor", lambda e, c=c: e.matmul(
                            banks[bi][:, :], lhsT=wg[s_][:, c, part * 128:(part + 1) * 128], rhs=nT[:, c, cols],
                            start=(c == 0), stop=(c == 7)), r=[wgb[s_], nTb[t][c]], w=[bankb[bi]])
                    tb = (qb_ if part == 0 else kb_)[s_][t]
                    tbB = (qB_ if part == 0 else kB_)[s_][t]
                    dA = (QA if part == 0 else KA)[s_]
                    dB = (QB if part == 0 else KB)[s_]
                    scale = 0.125 if part == 0 else 1.0
                    if is_diff or part == 0:
                        evac(True, dA[0:64, cols], banks[bi][0:64, :], scale, [bankb[bi]], [tb])
                        evac(False, dB[64:128, cols], banks[bi][64:128, :], scale, [bankb[bi]], [tbB])
                    else:
                        evac(t % 2 == 0, dA[:, cols], banks[bi][:, :], scale, [bankb[bi]], [tb])
                return c_qk
            for part in range(2):
                for t in range(NT):
                    chunks.append(mk_qk(part, t))

            def mk_v(t):
                def c_v():
                    bi = brot_.next()
                    for j in range(4):
                        blk = t * 4 + j
                        for c in range(8):
                            sc.op("tensor", lambda e, c=c, j=j, blk=blk: e.matmul(
                                banks[bi][:, j * 128:(j + 1) * 128], lhsT=nT[:, c, blk * 128:(blk + 1) * 128],
                                rhs=wg[s_][:, c, 256:384], start=(c == 0), stop=(c == 7)),
                                r=[wgb[s_], nTb[t][c]], w=[bankb[bi]])
                    evac(t % 2 == 1, V[s_][:, t * 4:(t + 1) * 4, :], banks[bi][:, :].rearrange("p (a b) -> p a b", a=4),
                         1.0, [bankb[bi]], [vb_[s_][t]])
                return c_v
            for t in range(NT):
                chunks.append(mk_v(t))
            return chunks

        def project(g):
            for ch in project_chunks(g, scorerot, True):
                ch()

        b7rot = Rot([7])
        proj_pending = []

        def proj_pop(n=1):
            for _ in range(n):
                if proj_pending:
                    proj_pending.pop(0)()

        prot = Rot([0, 1, 2, 3])
        deferred = []
        b_dummy = Buf("dummy")

        def pe_keepwarm(n):
            for _ in range(n):
                sc.op("tensor", lambda e: e.matmul(banks[7][:, :], lhsT=ones[:, :], rhs=ctab[:, 0:512], start=True, stop=True),
                      r=[b_ones, b_const], w=[bankb[7]])

        def run_deferred(force=False):
            keep = []
            for item in deferred:
                item[0] -= 1
                if item[0] <= 0 or force:
                    item[1]()
                else:
                    keep.append(item)
            deferred[:] = keep

        def c0_of(t, J):
            r = J - 4 * t
            return (max(r, 0) * 128, r)

        def head_norm_finish(chunk, t, src, srcb, lhs, inv_n, gain, sq_on_act=False):
            cols = slice(t * TW, (t + 1) * TW)
            if sq_on_act:
                sc.op("scalar", lambda e: e.activation(out=sq[:, 0, :], in_=src, func=AF.Square), r=[srcb], w=[sqbs[0]])
            else:
                sc.op("vector", lambda e: e.tensor_tensor(out=sq[:, 0, :], in0=src, in1=src, op=ALU.mult), r=[srcb], w=[sqbs[0]])

            def pe_part():
                sc.op("tensor", lambda e: e.matmul(banks[7][:, :], lhsT=lhs, rhs=sq[:, 0, :], start=True, stop=True),
                      r=[sqbs[0], b_ones], w=[bankb[7]])
                ri = rsrot.next()
                sc.op("scalar", lambda e: e.activation(out=ln_t[:, :], in_=banks[7][:, :], func=AF.Ln, bias=EPS, scale=inv_n),
                      r=[bankb[7]], w=[lnb])
                sc.op("scalar", lambda e: e.activation(out=rs_t[ri][:, :], in_=ln_t[:, :], func=AF.Exp, scale=-0.5),
                      r=[lnb], w=[rsb[ri]])
                sc.op("vector", lambda e: e.scalar_tensor_tensor(out=mixT[:, chunk, cols], in0=src, scalar=gain,
                                                                 in1=rs_t[ri][:, :], op0=ALU.mult, op1=ALU.mult),
                      r=[srcb, rsb[ri], b_lw, b_gv], w=[mixb[chunk][t]])
            deferred.append([6, pe_part])

        def diff_attention(h):
            s_ = h % 2
            units = []
            for t in range(NT):
                for m in range(2):
                    nJ = 4 * t + 4
                    for J in range(nJ):
                        units.append((t, m, J, nJ))
            acc_pairs = [(3, 4), (5, 6)]
            state = {}

            def stage_qk(u):
                t, m, J, nJ = u
                c0, r = c0_of(t, J)
                bi = scorerot.next()
                pi = prot.next()
                state[u] = (bi, pi)
                qs = slice(t * TW + c0, (t + 1) * TW)
                ks = slice(J * 128, (J + 1) * 128)
                if m == 0:
                    lhsT, rhs = KA[s_][:, ks], QA[s_][:, qs]
                else:
                    lhsT, rhs = KB[s_][:, ks], QB[s_][:, qs]
                rdb = [kb_[s_][J // 4], qb_[s_][t]] if m == 0 else [kB_[s_][J // 4], qB_[s_][t]]
                sc.op("tensor", lambda e: e.matmul(banks[bi][:, c0:TW], lhsT=lhsT, rhs=rhs, start=True, stop=(r < 0)),
                      r=rdb, w=[bankb[bi]])
                if r >= 0:
                    sc.op("tensor", lambda e: e.matmul(banks[bi][:, c0:c0 + 128], lhsT=ident, rhs=dtabs[h], start=False, stop=True),
                          r=[b_const], w=[bankb[bi]])
                sc.op("scalar", lambda e: e.activation(out=P[pi][:, c0:TW], in_=banks[bi][:, c0:TW], func=AF.Exp),
                      r=[bankb[bi]], w=[Pb[pi]])

            def stage_av(u):
                t, m, J, nJ = u
                c0, r = c0_of(t, J)
                bi, pi = state.pop(u)
                po, pl = acc_pairs[(t * 2 + m) % 2]
                sc.op("tensor", lambda e: e.matmul(banks[po][:, c0:TW], lhsT=V[s_][:, J, :], rhs=P[pi][:, c0:TW],
                                                   start=(J == 0), stop=(J == nJ - 1)),
                      r=[vb_[s_][J // 4], Pb[pi]], w=[bankb[po]])
                sc.op("tensor", lambda e: e.matmul(banks[pl][:, c0:TW], lhsT=ones[:, :], rhs=P[pi][:, c0:TW],
                                                   start=(J == 0), stop=(J == nJ - 1)),
                      r=[b_ones, Pb[pi]], w=[bankb[pl]])
                if J == nJ - 1:
                    am = af[m]
                    sc.op("scalar", lambda e: e.activation(out=rl, in_=banks[pl][:, :], func=AF.Ln), r=[bankb[pl]], w=[rlb])
                    sc.op("scalar", lambda e: e.activation(out=rl, in_=rl, func=AF.Exp, scale=-1.0), r=[rlb], w=[rlb])
                    sc.op("vector", lambda e: e.tensor_tensor(out=am, in0=banks[po][:, :], in1=rl, op=ALU.mult),
                          r=[bankb[po], rlb], w=[afb[m]])
                    if m == 1:
                        sc.op("vector", lambda e: e.scalar_tensor_tensor(out=af[2], in0=af[1], scalar=neglam, in1=af[0],
                                                                         op0=ALU.mult, op1=ALU.add),
                              r=[afb[0], afb[1], b_lw], w=[afb[2]])
                        head_norm_finish(h, t, af[2], afb[2], ones[:, :], 1.0 / 128.0, gsub)

            SK = 3
            n = len(units)
            proj_pending.extend(project_chunks(h + 1, b7rot, False))
            for i in range(n + SK):
                sc.group_begin("tensor")
                if i < n:
                    stage_qk(units[i])
                if i % 5 == 1:
                    proj_pop()
                if i - SK >= 0:
                    stage_av(units[i - SK])
                pe_keepwarm(DUMMY_DIFF)
                run_deferred()
                sc.group_end("tensor")
            proj_pop(100)

        sbscore = Rot([0, 1, 2, 3, 6])
        arot = Rot([0, 1, 2, 3, 4])
        erot = Rot([0, 1])

        def sb_items(p):
            return [(p, t, hh) for t in reversed(range(NT)) for hh in range(2)]

        sb_state = {"it": 0}

        def sb_pass1_units(item, idx):
            p, t, hh = item
            nJ = 4 * t + 4
            return [("p1", item, idx, J, nJ) for J in range(nJ)]

        def sb_pass2_units(item, idx):
            p, t, hh = item
            nJ = 4 * t + 4
            return [("p2", item, idx, J, nJ) for J in range(nJ)]

        drot = Rot([0, 1, 2])

        def sb_front(u):
            kind, (p, t, hh), idx, J, nJ = u
            s_ = p % 2
            b0 = 64 * hh
            c0, r = c0_of(t, J)
            qs = slice(t * TW + c0, (t + 1) * TW)
            ks = slice(J * 128, (J + 1) * 128)
            bi = sbscore.next()
            spi = idx % 2
            if kind == "p1":
                Qh = (QA if hh == 0 else QB)[s_]
                qhb = (qb_ if hh == 0 else qB_)[s_][t]
                mm = [(KA[s_][:, ks], Qh[:, qs], slice(c0, TW), [kb_[s_][J // 4], qhb])]
                if r >= 0:
                    mm.append((ident, mtab, slice(c0, c0 + 128), [b_const]))
            else:
                mm = [(negtri, SP[spi][:, J, c0:TW], slice(c0, TW), [b_const, SPb[spi][J]])]
                if J < nJ - 1:
                    mm.append((nu[:, J * 128:(J + 1) * 128], CS[spi][:, c0:TW], slice(c0, TW), [b_const, CSb[spi]]))
            ndup = DUMMY_SB if kind == "p1" else max(DUMMY_SB - 1, 0)
            for _ in range(ndup):
                lhsT, rhs, osl, rb = mm[0]
                sc.op("tensor", lambda e, lhsT=lhsT, rhs=rhs, osl=osl: e.matmul(
                    banks[bi][:, osl], lhsT=lhsT, rhs=rhs, start=True, stop=True), r=rb, w=[bankb[bi]])
            for i, (lhsT, rhs, osl, rb) in enumerate(mm):
                sc.op("tensor", lambda e, lhsT=lhsT, rhs=rhs, osl=osl, i=i: e.matmul(
                    banks[bi][:, osl], lhsT=lhsT, rhs=rhs, start=(i == 0), stop=(i == len(mm) - 1)),
                    r=rb, w=[bankb[bi]])
            if kind == "p1":
                sc.op("scalar", lambda e: e.activation(out=EB[spi][:, J, c0:TW], in_=banks[bi][:, c0:TW], func=AF.Exp),
                      r=[bankb[bi]], w=[EBb[spi][J]])

                def ln_part():
                    sc.op("scalar", lambda e: e.activation(out=SP[spi][:, J, c0:TW], in_=EB[spi][:, J, c0:TW], func=AF.Ln, bias=1.0),
                          r=[EBb[spi][J]], w=[SPb[spi][J]])
                return ln_part
            else:
                di = drot.next()
                ai = arot.next()
                sc.op("scalar", lambda e: e.activation(out=Dt[di][:, c0:TW], in_=banks[bi][:, c0:TW], func=AF.Exp),
                      r=[bankb[bi]], w=[Dtb[di]])
                sc.op("vector", lambda e: e.tensor_tensor(out=A[ai][:, c0:TW], in0=EB[spi][:, J, c0:TW], in1=Dt[di][:, c0:TW], op=ALU.mult),
                      r=[EBb[spi][J], Dtb[di]], w=[Ab[ai]])
                return ai

        def sb_back(u, ai):
            kind, (p, t, hh), idx, J, nJ = u
            s_ = p % 2
            b0 = 64 * hh
            c0, r = c0_of(t, J)
            spi = idx % 2
            if kind == "p1":
                sc.op("tensor", lambda e: e.matmul(banks[4][:, c0:TW], lhsT=esel[:, J * 128:(J + 1) * 128],
                                                   rhs=SP[spi][:, J, c0:TW], start=(J == 0), stop=(J == nJ - 1)),
                      r=[b_const, SPb[spi][J]], w=[bankb[4]])
                if J == nJ - 1:
                    sc.op("vector", lambda e: e.tensor_copy(out=CS[spi][0:16, :], in_=banks[4][0:16, :]),
                          r=[bankb[4]], w=[CSb[spi]])
            else:
                po = 5
                oi = 2 + (t % 2)
                sc.op("tensor", lambda e: e.matmul(banks[po][:, c0:TW], lhsT=V[s_][:, J, :],
                                                   rhs=A[ai][:, c0:TW], start=(J == 0), stop=(J == nJ - 1)),
                      r=[vb_[s_][J // 4], Ab[ai]], w=[bankb[po]])
                if J == nJ - 1:
                    sc.op("vector", lambda e: e.tensor_copy(out=af[oi][b0:b0 + 64, :], in_=banks[po][b0:b0 + 64, :]),
                          r=[bankb[po]], w=[afb[oi]])
                    if hh == 1:
                        head_norm_finish(4 + p, t, af[oi], afb[oi], bones[:, :], 1.0 / 64.0, gsb)

        def sb_attention_all():
            items = []
            for p in range(4):
                items += sb_items(p)
            n = len(items)
            SKB = 4
            LAG = 4
            pend = []
            for k in range(n + 1):
                l1 = sb_pass1_units(items[k], k) if k < n else []
                l2 = [None] * LAG + (sb_pass2_units(items[k - 1], k - 1) if k >= 1 else [])
                if k < n and k % 8 == 0:
                    proj_pop(100)
                    if items[k][0] == 3:
                        sc.dma("gpsimd", wo.rearrange("p a b -> p (a b)"), wout_d[:, :], woslot,
                               w=[wob] + [b for l in nTb for b in l])
                if k < n and k % 8 == 2 and items[k][0] + 1 < 4:
                    proj_pending.extend(project_chunks(4 + items[k][0] + 1, b7rot, False))
                step_in_item = 0
                for u1, u2 in zip_longest(l1, l2):
                    sc.group_begin("tensor")
                    cur = []
                    ln_part = None
                    if u1 is not None:
                        ln_part = sb_front(u1)
                        cur.append((u1, None))
                    if u2 is not None:
                        ai = sb_front(u2)
                        cur.append((u2, ai))
                    if ln_part is not None:
                        ln_part()
                    step_in_item += 1
                    if step_in_item % 3 == 2:
                        proj_pop()
                    pend.append(cur)
                    if len(pend) > SKB:
                        for (u, ai) in pend.pop(0):
                            sb_back(u, ai)
                    run_deferred()
                    sc.group_end("tensor")
            while pend:
                for (u, ai) in pend.pop(0):
                    sb_back(u, ai)
            run_deferred(force=True)
            run_deferred(force=True)

        project(0)
        for g in range(4):
            diff_attention(g)
        sc.fence(("scalar", "vector"), xsb)
        sc.fence(("scalar",), [b for l in kB_ for b in l] + Pb)
        sb_attention_all()
        sc.barrier()

        if debug and stop is None:
            dslot = sc.new_slot("dbg")
            dbgv = dbg_d.rearrange("(c p) t -> p c t", p=128)
            arena.reset()
            dstage = arena.alloc([8, S], F32)
            dsb = Buf("dstage")
            for c in range(8):
                sc.op("vector", lambda e, c=c: e.tensor_copy(out=dstage[:, c, :], in_=mixT[:, c, :]),
                      r=[mixb[c][t] for t in range(NT)], w=[dsb])
            sc.dma("sync", dbgv[:, :, :], dstage[:, :, :], dslot, r=[dsb])
            sc.barrier()

        arena.reset()
        hT = arena.alloc([8, S], F32)
        hb = [[Buf(f"h{c}_{t}") for t in range(NT)] for c in range(8)]
        hslot = [sc.new_slot(f"h{t}") for t in range(NT)]
        FG = 8
        wup = [arena.alloc([8, 512], BF16) for _ in range(2)]
        wdn = [arena.alloc([4, 1024], BF16) for _ in range(2)]
        wupb = [Buf("wup0"), Buf("wup1")]
        wdnb = [Buf("wdn0"), Buf("wdn1")]
        wupslot = [sc.new_slot("wup0"), sc.new_slot("wup1")]
        wdnslot = [sc.new_slot("wdn0"), sc.new_slot("wdn1")]
        uT = [arena.alloc([4, TW], BF16) for _ in range(2)]
        ub = [[Buf(f"u{i}_{f}") for f in range(4)] for i in range(2)]
        rr = [arena.alloc([1, TW], F32)[:, 0, :] for _ in range(2)]
        rrb = [Buf("rr0"), Buf("rr1")]

        for t in range(NT):
            cols = slice(t * TW, (t + 1) * TW)
            sc.dma("sync", hT[:, :, cols], xTv[:, :, cols], hslot[t], w=[hb[c][t] for c in range(8)])

        def load_ffn_group(G):
            k = G % 2
            sc.dma("gpsimd", wup[k].rearrange("p a b -> p (a b)"), wup_d[G, :, :], wupslot[k], w=[wupb[k]])
            sc.dma("gpsimd", wdn[k].rearrange("p a b -> p (a b)"), wdn_d[G, :, :], wdnslot[k], w=[wdnb[k]])

        load_ffn_group(0)
        load_ffn_group(1)

        brot = Rot([0, 1, 2, 3, 4, 5, 6])
        for t in range(NT):
            cols = slice(t * TW, (t + 1) * TW)
            for dc in range(8):
                bi = brot.next()
                for ec in range(8):
                    sc.op("tensor", lambda e, bi=bi, ec=ec, dc=dc, cols=cols: e.matmul(
                        banks[bi][:, :], lhsT=wo[:, ec, dc * 128:(dc + 1) * 128], rhs=mixT[:, ec, cols],
                        start=(ec == 0), stop=(ec == 7)), r=[wob, mixb[ec][t]], w=[bankb[bi]])
                sc.op("vector", lambda e, bi=bi, dc=dc, cols=cols: e.tensor_tensor(
                    out=hT[:, dc, cols], in0=banks[bi][:, :], in1=hT[:, dc, cols], op=ALU.add),
                    r=[bankb[bi]], w=[hb[dc][t]])

        def dump_direct(src3d, bufs):
            dslot2 = sc.new_slot("dbg2")
            dbgv2 = dbg_d.rearrange("(c p) t -> p c t", p=128)
            sc.dma("sync", dbgv2[:, :, :], src3d[:, :, :], dslot2, r=bufs)
            sc.barrier()
            block = es.enter_context(nc.Block())
            sc.emit(block)
            return nc

        if stop == "C":
            return dump_direct(hT, [b for l in hb for b in l])

        sc.fence(("vector",), [wob])
        for t in range(NT):
            cols = slice(t * TW, (t + 1) * TW)
            rs, rb = rms_stats(lambda c, cols=cols: hT[:, c, cols], [hb[c][t] for c in range(8)], 8, 7, 1.0 / D, ones[:, :])
            for c in range(8):
                sc.op("vector", lambda e, c=c, rs=rs, cols=cols: e.scalar_tensor_tensor(
                    out=nT[:, c, cols], in0=hT[:, c, cols], scalar=g2(c), in1=rs[:, :], op0=ALU.mult, op1=ALU.mult),
                    r=[hb[c][t], rb, b_gv], w=[nTb[t][c]])

        sc.fence(("vector",), [b for l in mixb for b in l])
        ost = [mixT[:, 2 * i:2 * i + 2, :].rearrange("p a b -> p (a b)").bitcast(F32).rearrange("p (a b) -> p a b", a=4)
               for i in range(4)]
        ostb = [Buf(f"ost{i}") for i in range(4)]
        oslot = [sc.new_slot(f"o{i}") for i in range(4)]
        orot = Rot([0, 1, 2, 3])
        def final_tile(t):
            cols = slice(t * TW, (t + 1) * TW)
            rs, rb = rms_stats(lambda c, cols=cols: hT[:, c, cols], [hb[c][t] for c in range(8)], 8, 7, 1.0 / D, ones[:, :])
            for half in range(2):
                oi = orot.next()
                for cc in range(4):
                    c = half * 4 + cc
                    sc.op("vector", lambda e, c=c, cc=cc, oi=oi, rs=rs, cols=cols: e.scalar_tensor_tensor(
                        out=ost[oi][:, cc, :], in0=hT[:, c, cols], scalar=gf(c), in1=rs[:, :], op0=ALU.mult, op1=ALU.mult),
                        r=[hb[c][t], rb, b_gv], w=[ostb[oi]])
                sc.dma("sync", outTv[:, half * 4:(half + 1) * 4, cols], ost[oi][:, :, :], oslot[oi], r=[ostb[oi]])

        urot = Rot([0, 1])
        rrot = Rot([0, 1])
        def ffn_up(G, t, ui):
            k = G % 2
            cols = slice(t * TW, (t + 1) * TW)
            for fc in range(4):
                bi = brot.next()
                for dc in range(8):
                    sc.op("tensor", lambda e, bi=bi, fc=fc, dc=dc: e.matmul(
                        banks[bi][:, :], lhsT=wup[k][:, dc, fc * 128:(fc + 1) * 128], rhs=nT[:, dc, cols],
                        start=(dc == 0), stop=(dc == 7)), r=[wupb[k], nTb[t][dc]], w=[bankb[bi]])
                ri = rrot.next()
                sc.op("scalar", lambda e, bi=bi, ri=ri: e.activation(out=rr[ri], in_=banks[bi][:, :], func=AF.Relu),
                      r=[bankb[bi]], w=[rrb[ri]])
                sc.op("vector", lambda e, ri=ri, fc=fc: e.tensor_tensor(
                    out=uT[ui][:, fc, :], in0=rr[ri], in1=rr[ri], op=ALU.mult), r=[rrb[ri]], w=[ub[ui][fc]])

        def ffn_down(G, t, ui):
            k = G % 2
            cols = slice(t * TW, (t + 1) * TW)
            for dc in range(8):
                bi = brot.next()
                for fc in range(4):
                    sc.op("tensor", lambda e, bi=bi, fc=fc, dc=dc: e.matmul(
                        banks[bi][:, :], lhsT=wdn[k][:, fc, dc * 128:(dc + 1) * 128], rhs=uT[ui][:, fc, :],
                        start=(fc == 0), stop=(fc == 3)), r=[wdnb[k], ub[ui][fc]], w=[bankb[bi]])
                sc.op("vector", lambda e, bi=bi, dc=dc: e.tensor_tensor(
                    out=hT[:, dc, cols], in0=banks[bi][:, :], in1=hT[:, dc, cols], op=ALU.add),
                    r=[bankb[bi]], w=[hb[dc][t]])

        seq = [(G, t) for G in range(FG) for t in range(NT)]
        uis = [i % 2 for i in range(len(seq))]
        for i in range(len(seq) + 1):
            if i < len(seq):
                ffn_up(seq[i][0], seq[i][1], uis[i])
            if i >= 1:
                G, t = seq[i - 1]
                ffn_down(G, t, uis[i - 1])
                if G == FG - 1 and t >= 1:
                    final_tile(t - 1)
                if t == NT - 1 and G + 2 < FG:
                    load_ffn_group(G + 2)

        final_tile(NT - 1)

        sc.barrier()

        block = es.enter_context(nc.Block())
        sc.emit(block)
    return nc


def _const_tables():
    ct = np.zeros((128, NCT), np.float32)
    ct[:, 0:128] = np.eye(128, dtype=np.float32)
    j = np.arange(128)[:, None]
    s = np.arange(128)[None, :]
    ct[:, 128:256] = np.where(j >= s, -1.0, 0.0)
    ss_ = np.arange(128)[:, None]
    tt = np.arange(128)[None, :]
    ct[:, 256:384] = np.where(ss_ < tt, 0.0, NEG)
    for h in range(4):
        sl = SLOPES[h]
        d = np.where(ss_ <= tt, 0.0, np.where((ss_ // 64) <= (tt // 64), -2.0 * sl * (ss_ - tt), NEG))
        ct[:, 384 + 128 * h:512 + 128 * h] = d
    for Jp in range(16):
        ct[:, 896 + Jp * 128 + Jp] = 1.0
    nu = np.zeros((128, NB, 128), np.float32)
    for Jp in range(16):
        for J in range(16):
            if Jp > J:
                nu[Jp, J, :] = -1.0
    t = np.arange(S)
    qaug = np.stack([-(t % 128), -128.0 * (t // 128), np.ones(S), np.ones(S)]).astype(np.float32)
    kaug = np.zeros((4, 4, S), np.float32)
    for h in range(4):
        sl = SLOPES[h]
        kaug[h] = np.stack([sl * np.ones(S), sl * np.ones(S), sl * (t % 128), sl * 128.0 * (t // 128)])
    return ct, nu.reshape(128, NB * 128), qaug, kaug


def _prep_shared(norm1_g, w_in, lambda_q1, lambda_k1, lambda_q2, lambda_k2, diff_subln_g, sb_norm_g,
                 w_out, norm2_g, w_up, w_down, final_norm_g):
    f = np.float32
    w_in = np.asarray(w_in, f)[0]
    groups = []
    for g in range(8):
        if g < 4:
            qc, kc, vc = g * 128, 512 + g * 128, 1024 + g * 128
        else:
            p = g - 4
            qc, kc, vc = 1536 + p * 128, 2048 + p * 128, 2560 + p * 128
        wgm = np.concatenate([w_in[:, qc:qc + 128], w_in[:, kc:kc + 128], w_in[:, vc:vc + 128]], axis=1)
        groups.append(wgm.reshape(8, 128, 384).transpose(1, 0, 2).reshape(128, 8 * 384))
    win = np.ascontiguousarray(np.stack(groups))
    wout = np.ascontiguousarray(np.asarray(w_out, f)[0].reshape(8, 128, 1024).transpose(1, 0, 2).reshape(128, 8 * 1024))
    wu = np.asarray(w_up, f)[0]
    wup = np.ascontiguousarray(np.stack([
        wu[:, G * 512:(G + 1) * 512].reshape(8, 128, 512).transpose(1, 0, 2).reshape(128, 8 * 512) for G in range(8)]))
    wd = np.asarray(w_down, f)[0]
    wdn = np.ascontiguousarray(np.stack([
        wd[G * 512:(G + 1) * 512, :].reshape(4, 128, 1024).transpose(1, 0, 2).reshape(128, 4 * 1024) for G in range(8)]))
    gv = np.zeros((128, 26), f)
    gv[:, 0:8] = np.asarray(norm1_g, f)[0].reshape(8, 128).T
    gv[:, 8:16] = np.asarray(norm2_g, f)[0].reshape(8, 128).T
    gv[:, 16:24] = np.asarray(final_norm_g, f).reshape(8, 128).T
    gv[:, 24] = np.asarray(diff_subln_g, f)[0]
    gv[:, 25] = np.concatenate([np.asarray(sb_norm_g, f)[0]] * 2)
    lamv = np.concatenate([np.asarray(a, f)[0] for a in (lambda_q1, lambda_k1, lambda_q2, lambda_k2)])
    lamv = np.ascontiguousarray(np.broadcast_to(lamv[None, :], (128, 256)))
    ct, nu, qaug, kaug = _const_tables()
    return {"win": win, "wout": wout, "wup": wup, "wdn": wdn, "gv": gv, "lamv": lamv,
            "ctab": ct, "nutab": nu, "qaug": qaug, "kaug": kaug}


_CACHE = {}


def kernel(x, norm1_g, w_in, lambda_q1, lambda_k1, lambda_q2, lambda_k2, diff_subln_g, sb_norm_g,
           w_out, norm2_g, w_up, w_down, final_norm_g):
    debug = bool(int(os.environ.get("MK_DEBUG", "0")))
    x = np.asarray(x, np.float32)
    shared = _prep_shared(norm1_g, w_in, lambda_q1, lambda_k1, lambda_q2, lambda_k2, diff_subln_g, sb_norm_g,
                          w_out, norm2_g, w_up, w_down, final_norm_g)
    n = x.shape[0]
    in_maps = []
    for b in range(n):
        m = dict(shared)
        m["xT"] = np.ascontiguousarray(x[b].T)
        in_maps.append(m)
    key = ("nc", debug)
    if key not in _CACHE:
        _CACHE[key] = build_program(debug=debug)
    nc = _CACHE[key]
    res = run_bass_kernel_spmd(nc, in_maps, core_ids=list(range(n)))
    out = np.stack([np.ascontiguousarray(r["outT"].T) for r in res.results]).astype(np.float32)
    if debug:
        kernel.dbg = np.stack([r["dbg"] for r in res.results])
    return out
```

```python
import os
from contextlib import ExitStack
from itertools import zip_longest

import numpy as np
import concourse.bass as bass
import concourse.mybir as mybir
from concourse.bass_utils import run_bass_kernel_spmd

F32 = mybir.dt.float32
BF16 = mybir.dt.bfloat16
AF = mybir.ActivationFunctionType
ALU = mybir.AluOpType
AX = mybir.AxisListType

D = 1024
S = 2048
NT = 4
TW = 512
NB = 16
EPS = 1e-6
NEG = -30000.0
SLOPES = [2.0 ** (-8.0 * (h + 1) / 4.0) for h in range(4)]
LAMBDA_INIT = 0.8 - 0.6

ENGINES = ("tensor", "scalar", "vector", "gpsimd", "sync")
SAME_ENGINE_SYNC = True
DUMMY_SB = int(os.environ.get('MK_DUMMY_SB', '0'))
DUMMY_DIFF = int(os.environ.get('MK_DUMMY_DIFF', '0'))


class Buf:
    __slots__ = ("name", "w", "r")

    def __init__(self, name=""):
        self.name = name
        self.w = None
        self.r = []


class Op:
    __slots__ = ("eng", "fn", "deps", "signal", "sigval", "dma_slot", "dma_val", "hoist", "seq")

    def __init__(self, eng, fn):
        self.eng = eng
        self.fn = fn
        self.deps = []
        self.signal = False
        self.sigval = None
        self.dma_slot = None
        self.dma_val = None
        self.hoist = False
        self.seq = -1


class DmaSlot:
    def __init__(self, sem):
        self.sem = sem
        self.count = 0


class Sched:
    def __init__(self, nc, es):
        self.nc = nc
        self.es = es
        self.ops = {e: [] for e in ENGINES}
        self.sems = {e: es.enter_context(nc.semaphore("s_" + e)) for e in ENGINES}
        self.slots = []

    def new_slot(self, name):
        s = DmaSlot(self.es.enter_context(self.nc.semaphore("d_" + name)))
        self.slots.append(s)
        return s

    def _add_dep(self, op, dep):
        if dep is None or dep is op:
            return
        if dep.fn is None and dep.dma_slot is None:
            return
        if dep.dma_slot is None:
            if dep.eng == op.eng:
                if op.eng == "tensor" or not SAME_ENGINE_SYNC:
                    return
            dep.signal = True
        op.deps.append(dep)

    def op(self, eng, fn, r=(), w=()):
        o = Op(eng, fn)
        self.seqc = getattr(self, "seqc", 0) + 1
        o.seq = self.seqc
        for b in r:
            self._add_dep(o, b.w)
        for b in w:
            self._add_dep(o, b.w)
            for rd in b.r:
                self._add_dep(o, rd)
        for b in w:
            b.w = o
            b.r = []
        for b in r:
            if b not in w:
                b.r.append(o)
        self.ops[eng].append(o)
        return o

    def group_begin(self, eng):
        self._gstart = (eng, len(self.ops[eng]))
        self._gseq = getattr(self, "seqc", 0)

    def group_end(self, eng):
        e, start = self._gstart
        assert e == eng
        grp = self.ops[eng][start:]
        if len(grp) < 2:
            return
        o = Op(eng, None)
        o.hoist = True
        for g in grp:
            o.deps.extend(d for d in g.deps if 0 <= d.seq <= self._gseq)
        self.ops[eng].insert(start, o)

    def fence(self, engs, bufs):
        for e in engs:
            o = Op(e, None)
            for b in bufs:
                self._add_dep(o, b.w)
                for rd in b.r:
                    self._add_dep(o, rd)
            self.ops[e].append(o)

    def dma(self, eng, out, in_, slot, r=(), w=()):
        o = self.op(eng, lambda e: e.dma_start(out=out, in_=in_), r=r, w=w)
        slot.count += 16
        o.dma_slot = slot
        o.dma_val = slot.count
        return o

    def barrier(self):
        lasts = []
        for e in ENGINES:
            real = [o for o in self.ops[e] if o.fn is not None]
            if real:
                lasts.append(real[-1])
        for e in ENGINES:
            o = Op(e, None)
            for l in lasts:
                if l.eng != e:
                    if l.dma_slot is None:
                        l.signal = True
                    o.deps.append(l)
            for sl in self.slots:
                if sl.count:
                    d = Op("dma", None)
                    d.dma_slot = sl
                    d.dma_val = sl.count
                    o.deps.append(d)
            self.ops[e].append(o)

    def emit(self, block):
        for e in ENGINES:
            c = 0
            for o in self.ops[e]:
                if o.signal and o.fn is not None:
                    c += 1
                    o.sigval = c
        sems = self.sems

        def run(engname):
            def body(eng):
                waited = {}
                for o in self.ops[engname]:
                    need = {}
                    for d in o.deps:
                        if d.dma_slot is not None:
                            sem, val = d.dma_slot.sem, d.dma_val
                        else:
                            sem, val = sems[d.eng], d.sigval
                        key = id(sem)
                        if key not in need or need[key][1] < val:
                            need[key] = (sem, val)
                    for key, (sem, val) in need.items():
                        if waited.get(key, 0) < val:
                            eng.wait_ge(sem, val)
                            waited[key] = val
                    if o.fn is None:
                        continue
                    ins = o.fn(eng)
                    if o.dma_slot is not None:
                        ins.then_inc(o.dma_slot.sem, 16)
                    elif o.signal:
                        ins.then_inc(sems[engname], 1)
            return body

        block.tensor(run("tensor"))
        block.scalar(run("scalar"))
        block.vector(run("vector"))
        block.gpsimd(run("gpsimd"))
        block.sync(run("sync"))


class Arena:
    def __init__(self, ap, words):
        self.ap = ap
        self.words = words
        self.off = 0

    def reset(self):
        self.off = 0

    def alloc(self, shape, dtype):
        n = int(np.prod(shape))
        words = n if dtype == F32 else (n + 1) // 2
        assert self.off + words <= self.words, (self.off, words, self.words)
        a = self.ap[:, self.off:self.off + words]
        self.off += words
        if dtype != F32:
            a = a.bitcast(dtype)
        if len(shape) == 2:
            a = a.rearrange("p (a b) -> p a b", a=shape[0])
        elif len(shape) == 3:
            a = a.rearrange("p (a b c) -> p a b c", a=shape[0], b=shape[1])
        return a


class Rot:
    def __init__(self, items):
        self.items = list(items)
        self.i = 0

    def next(self):
        x = self.items[self.i % len(self.items)]
        self.i += 1
        return x


NCT = 7 * 128 + 16 * 128
ARENA_WORDS = 116 * 256


def build_program(debug=False, stop=None):
    nc = bass.Bass("TRN2", target_bir_lowering=False)
    dt = nc.dram_tensor
    xT_d = dt("xT", [D, S], F32, kind="ExternalInput").ap()
    win_d = dt("win", [8, 128, 8 * 384], F32, kind="ExternalInput").ap()
    wout_d = dt("wout", [128, 8 * 1024], F32, kind="ExternalInput").ap()
    wup_d = dt("wup", [8, 128, 8 * 512], F32, kind="ExternalInput").ap()
    wdn_d = dt("wdn", [8, 128, 4 * 1024], F32, kind="ExternalInput").ap()
    gv_d = dt("gv", [128, 26], F32, kind="ExternalInput").ap()
    lam_d = dt("lamv", [128, 256], F32, kind="ExternalInput").ap()
    ctab_d = dt("ctab", [128, NCT], F32, kind="ExternalInput").ap()
    nu_d = dt("nutab", [128, NB * 128], F32, kind="ExternalInput").ap()
    qaug_d = dt("qaug", [4, S], F32, kind="ExternalInput").ap()
    kaug_d = dt("kaug", [4, 4, S], F32, kind="ExternalInput").ap()
    outT_d = dt("outT", [D, S], F32, kind="ExternalOutput").ap()
    dbg_d = None
    if debug:
        dbg_d = dt("dbg", [D, S], F32, kind="ExternalOutput").ap()

    xTv = xT_d.rearrange("(c p) t -> p c t", p=128)
    outTv = outT_d.rearrange("(c p) t -> p c t", p=128)

    with ExitStack() as es:
        sb = lambda name, shape, dtype: es.enter_context(nc.sbuf_tensor(name, shape, dtype))
        nT = sb("nT", [128, 8, S], BF16)
        mixT = sb("mixT", [128, 8, S], BF16)
        sq = sb("sq", [128, 8, TW], BF16)
        rs_t = [sb(f"rs{i}", [128, TW], F32) for i in range(2)]
        ln_t = sb("lnv", [128, TW], F32)
        ctab = sb("ctab_sb", [128, NCT], BF16)
        ones = sb("ones", [128, 128], BF16)
        bones = sb("bones", [128, 128], BF16)
        nu = sb("nu", [128, NB * 128], BF16)
        gv = sb("gv_sb", [128, 26], F32)
        lamt = sb("lamt", [128, 256], F32)
        lamw = sb("lamw", [128, 128], F32)
        lams = sb("lams", [128, 8], F32)
        arena_t = sb("arena", [128, ARENA_WORDS], F32)
        arena = Arena(arena_t[:, :], ARENA_WORDS)
        banks = [es.enter_context(nc.psum_tensor(f"pb{i}", [128, TW], F32)) for i in range(8)]
        bankb = [Buf(f"bank{i}") for i in range(8)]

        sc = Sched(nc, es)
        ident = ctab[:, 0:128]
        negtri = ctab[:, 128:256]
        mtab = ctab[:, 256:384]
        dtabs = [ctab[:, 384 + 128 * h: 512 + 128 * h] for h in range(4)]
        esel = ctab[:, 896:896 + 16 * 128]
        g1 = lambda c: gv[:, c:c + 1]
        g2 = lambda c: gv[:, 8 + c:9 + c]
        gf = lambda c: gv[:, 16 + c:17 + c]
        gsub = lams[:, 4:5]
        gsb = gv[:, 25:26]
        neglam = lams[:, 3:4]

        b_const = Buf("const")
        b_gv = Buf("gv")
        b_lam = Buf("lam")
        s_c = [sc.new_slot(f"c{i}") for i in range(4)]
        sc.dma("gpsimd", ctab[:, :], ctab_d[:, :], s_c[0], w=[b_const])
        sc.dma("gpsimd", nu[:, :], nu_d[:, :], s_c[1], w=[b_const])
        sc.dma("sync", gv[:, :], gv_d[:, :], s_c[2], w=[b_gv])
        sc.dma("sync", lamt[:, :], lam_d[:, :], s_c[3], w=[b_lam])
        b_ones = Buf("ones")
        sc.op("gpsimd", lambda e: e.memset(ones[:, :], 1.0), w=[b_ones])
        sc.op("gpsimd", lambda e: e.memset(bones[:, :], 0.0), w=[b_ones])
        sc.op("gpsimd", lambda e: e.memset(bones[0:64, 0:64], 1.0), w=[b_ones])
        sc.op("gpsimd", lambda e: e.memset(bones[64:128, 64:128], 1.0), w=[b_ones])
        b_lw = Buf("lamw")
        sc.op("vector", lambda e: e.tensor_tensor(out=lamw[:, 0:64], in0=lamt[:, 0:64], in1=lamt[:, 64:128], op=ALU.mult), r=[b_lam], w=[b_lw])
        sc.op("vector", lambda e: e.tensor_tensor(out=lamw[:, 64:128], in0=lamt[:, 128:192], in1=lamt[:, 192:256], op=ALU.mult), r=[b_lam], w=[b_lw])
        sc.op("vector", lambda e: e.tensor_reduce(out=lams[:, 0:1], in_=lamw[:, 0:64], axis=AX.X, op=ALU.add), r=[b_lw], w=[b_lw])
        sc.op("vector", lambda e: e.tensor_reduce(out=lams[:, 1:2], in_=lamw[:, 64:128], axis=AX.X, op=ALU.add), r=[b_lw], w=[b_lw])
        sc.op("scalar", lambda e: e.activation(out=lams[:, 0:2], in_=lams[:, 0:2], func=AF.Exp), r=[b_lw], w=[b_lw])
        sc.op("vector", lambda e: e.tensor_tensor(out=lams[:, 2:3], in0=lams[:, 1:2], in1=lams[:, 0:1], op=ALU.subtract), r=[b_lw], w=[b_lw])
        sc.op("vector", lambda e: e.tensor_scalar(out=lams[:, 3:4], in0=lams[:, 2:3], scalar1=-LAMBDA_INIT, scalar2=None, op0=ALU.add), r=[b_lw], w=[b_lw])
        sc.op("vector", lambda e: e.tensor_scalar(out=lams[:, 4:5], in0=gv[:, 24:25], scalar1=1.0 - LAMBDA_INIT, scalar2=None, op0=ALU.mult), r=[b_gv, b_lw], w=[b_lw])

        nTb = [[Buf(f"nT{t}_{c}") for c in range(8)] for t in range(NT)]
        mixb = [[Buf(f"mix{c}_{t}") for t in range(NT)] for c in range(8)]
        sqbs = [Buf(f"sq{c}") for c in range(8)]
        rsb = [Buf("rs0"), Buf("rs1")]
        lnb = Buf("lnv")
        rsrot = Rot([0, 1])

        def rms_stats(src_c, src_bufs, nch, bank_i, inv_n, lhs_ones):
            for c in range(nch):
                sc.op("scalar", lambda e, c=c: e.activation(out=sq[:, c, :], in_=src_c(c), func=AF.Square),
                      r=src_bufs, w=[sqbs[c]])
            for c in range(nch):
                sc.op("tensor", lambda e, c=c: e.matmul(banks[bank_i][:, :], lhsT=lhs_ones, rhs=sq[:, c, :],
                                                        start=(c == 0), stop=(c == nch - 1)),
                      r=[sqbs[c], b_ones], w=[bankb[bank_i]])
            ri = rsrot.next()
            sc.op("scalar", lambda e: e.activation(out=ln_t[:, :], in_=banks[bank_i][:, :], func=AF.Ln, bias=EPS, scale=inv_n),
                  r=[bankb[bank_i]], w=[lnb])
            sc.op("scalar", lambda e: e.activation(out=rs_t[ri][:, :], in_=ln_t[:, :], func=AF.Exp, scale=-0.5),
                  r=[lnb], w=[rsb[ri]])
            return rs_t[ri], rsb[ri]

        arena.reset()
        CS_ = [arena.alloc([1, TW], BF16) for _ in range(2)]
        rl = arena.alloc([1, TW], F32)[:, 0, :]
        rlb = Buf("rl")
        wg1 = arena.alloc([8, 384], BF16)
        wg = [wg1, wg1]
        wgb1 = Buf("wg")
        wgb = [wgb1, wgb1]
        wgs1 = sc.new_slot("wg")
        wgslot = [wgs1, wgs1]
        V = [arena.alloc([NB, 128], BF16) for _ in range(2)]
        QA = [arena.alloc([1, S], BF16) for _ in range(2)]
        KA = [arena.alloc([1, S], BF16) for _ in range(2)]
        QB = [arena.alloc([1, S], BF16) for _ in range(2)]
        u_off = arena.off
        KB = [arena.alloc([1, S], BF16) for _ in range(2)]
        P_ = [arena.alloc([1, TW], BF16) for _ in range(4)]
        arena.alloc([4, TW], BF16)
        EB = [Arena(arena.ap[:, u_off:u_off + 4096], 4096).alloc([NB, TW], BF16), arena.alloc([NB, TW], BF16)]
        EBb = [[Buf(f"EB{i}_{j}") for j in range(NB)] for i in range(2)]
        QA = [a[:, 0, :] for a in QA]; QB = [a[:, 0, :] for a in QB]
        KA = [a[:, 0, :] for a in KA]; KB = [a[:, 0, :] for a in KB]
        CS = [a[:, 0, :] for a in CS_]
        CSb = [Buf("CS0"), Buf("CS1")]
        qb_ = [[Buf(f"q{s}_{t}") for t in range(NT)] for s in range(2)]
        kb_ = [[Buf(f"k{s}_{t}") for t in range(NT)] for s in range(2)]
        qB_ = [[Buf(f"qB{s}_{t}") for t in range(NT)] for s in range(2)]
        kB_ = [[Buf(f"kB{s}_{t}") for t in range(NT)] for s in range(2)]
        vb_ = [[Buf(f"v{s}_{t}") for t in range(NT)] for s in range(2)]
        augslot = [[sc.new_slot(f"aug{i}_{j}") for j in range(2)] for i in range(4)]
        P = [a[:, 0, :] for a in P_]
        Pb = [Buf(f"P{i}") for i in range(4)]
        SP = [arena.alloc([NB, TW], BF16) for _ in range(2)]
        SPb = [[Buf(f"SP{i}_{j}") for j in range(NB)] for i in range(2)]
        Dt = [arena.alloc([1, TW], BF16)[:, 0, :] for _ in range(3)]
        Dtb = [Buf(f"Dt{i}") for i in range(3)]
        A = [arena.alloc([1, TW], BF16)[:, 0, :] for _ in range(5)]
        Ab = [Buf(f"A{i}") for i in range(5)]
        af = [sq[:, 2 + 2 * i:4 + 2 * i, :].rearrange("p a b -> p (a b)").bitcast(F32) for i in range(3)]
        af.append(arena.alloc([1, TW], F32)[:, 0, :])
        afb = [Buf(f"af{i}") for i in range(4)]

        for s_ in range(2):
            sc.op("gpsimd", lambda e, s_=s_: e.memset(QB[s_][0:64, :], 0.0), w=qB_[s_])
            sc.op("gpsimd", lambda e, s_=s_: e.memset(KB[s_][0:64, :], 0.0), w=kB_[s_])
            sc.op("gpsimd", lambda e, s_=s_: e.memset(QA[s_][64:128, :], 0.0), w=qb_[s_])
            sc.op("gpsimd", lambda e, s_=s_: e.memset(KA[s_][64:128, :], 0.0), w=kb_[s_])
            sc.op("gpsimd", lambda e, s_=s_: e.memset(CS[s_][:, :], 0.0), w=[CSb[s_]])
            sc.dma("gpsimd", QA[s_][64:68, :], qaug_d[:, :], augslot[0][s_], w=qb_[s_])
            sc.dma("gpsimd", QB[s_][60:64, :], qaug_d[:, :], augslot[1][s_], w=qB_[s_])

        NXS = 3
        xar = Arena(arena.ap[:, ARENA_WORDS - NXS * 4096:ARENA_WORDS], NXS * 4096)
        xs = [xar.alloc([8, TW], F32) for _ in range(NXS)]
        xsb = [Buf(f"xs{i}") for i in range(NXS)]
        xslot = [sc.new_slot(f"x{i}") for i in range(NXS)]
        for t in range(NT):
            k = t % NXS
            cols = slice(t * TW, (t + 1) * TW)
            sc.dma("sync", xs[k][:, :, :], xTv[:, :, cols], xslot[k], w=[xsb[k]])
            rs, rb = rms_stats(lambda c, k=k: xs[k][:, c, :], [xsb[k]], 8, 7, 1.0 / D, ones[:, :])
            for c in range(8):
                sc.op("vector", lambda e, c=c, k=k, rs=rs, cols=cols: e.scalar_tensor_tensor(
                    out=nT[:, c, cols], in0=xs[k][:, c, :], scalar=g1(c), in1=rs[:, :], op0=ALU.mult, op1=ALU.mult),
                    r=[xsb[k], rb, b_gv], w=[nTb[t][c]])

        def dump_and_finish(src3d, bufs):
            dslot = sc.new_slot("dbg")
            dbgv = dbg_d.rearrange("(c p) t -> p c t", p=128)
            arena.reset()
            dstage = arena.alloc([8, S], F32)
            dsb = Buf("dstage")
            for c in range(8):
                sc.op("vector", lambda e, c=c: e.tensor_copy(out=dstage[:, c, :], in_=src3d[:, c, :]), r=bufs, w=[dsb])
            sc.dma("sync", dbgv[:, :, :], dstage[:, :, :], dslot, r=[dsb])
            sc.barrier()
            block = es.enter_context(nc.Block())
            sc.emit(block)
            return nc

        if stop == "A":
            return dump_and_finish(nT, [b for l in nTb for b in l])


        wo = nT[:, 0:4, :].rearrange("p a b -> p (a b)").rearrange("p (a b) -> p a b", a=8)
        wob = Buf("wo")
        woslot = sc.new_slot("wo")
        scorerot = Rot([0, 1, 2])

        def project_chunks(g, brot_, act_ok):
            s_ = g % 2
            is_diff = g < 4
            chunks = []

            def c_dma():
                sc.dma("gpsimd", wg[s_].rearrange("p a b -> p (a b)"), win_d[g, :, :], wgslot[s_], w=[wgb[s_]])
                if is_diff:
                    sc.dma("gpsimd", KA[s_][64:68, :], kaug_d[g, :, :], augslot[2][s_], w=kb_[s_])
                    sc.dma("gpsimd", KB[s_][60:64, :], kaug_d[g, :, :], augslot[3][s_], w=kB_[s_])
                elif g in (4, 5):
                    sc.op("gpsimd", lambda e: e.memset(QA[s_][64:128, :], 0.0), w=qb_[s_])
                    sc.op("gpsimd", lambda e: e.memset(QB[s_][0:64, :], 0.0), w=qB_[s_])
            chunks.append(c_dma)

            def evac(use_act, out, in_, scale, rb, wb):
                if (use_act and act_ok) or act_ok == "all":
                    sc.op("scalar", lambda e: e.activation(out=out, in_=in_, func=AF.Copy, scale=scale), r=rb, w=wb)
                else:
                    sc.op("vector", lambda e: e.tensor_scalar(out=out, in0=in_, scalar1=scale, scalar2=None, op0=ALU.mult),
                          r=rb, w=wb)

            def mk_qk(part, t):
                def c_qk():
                    bi = brot_.next()
                    cols = slice(t * TW, (t + 1) * TW)
                    for c in range(8):
                        sc.op("tensor", lambda e, c=c: e.matmul(
                            banks[bi][:, :], lhsT=wg[s_][:, c, part * 128:(part + 1) * 128], rhs=nT[:, c, cols],
                            start=(c == 0), stop=(c == 7)), r=[wgb[s_], nTb[t][c]], w=[bankb[bi]])
                    tb = (qb_ if part == 0 else kb_)[s_][t]
                    tbB = (qB_ if part == 0 else kB_)[s_][t]
                    dA = (QA if part == 0 else KA)[s_]
                    dB = (QB if part == 0 else KB)[s_]
                    scale = 0.125 if part == 0 else 1.0
                    if is_diff or part == 0:
                        evac(True, dA[0:64, cols], banks[bi][0:64, :], scale, [bankb[bi]], [tb])
                        evac(False, dB[64:128, cols], banks[bi][64:128, :], scale, [bankb[bi]], [tbB])
                    else:
                        evac(t % 2 == 0, dA[:, cols], banks[bi][:, :], scale, [bankb[bi]], [tb])
                return c_qk
            for part in range(2):
                for t in range(NT):
                    chunks.append(mk_qk(part, t))

            def mk_v(t):
                def c_v():
                    bi = brot_.next()
                    for j in range(4):
                        blk = t * 4 + j
                        for c in range(8):
                            sc.op("tensor", lambda e, c=c, j=j, blk=blk: e.matmul(
                                banks[bi][:, j * 128:(j + 1) * 128], lhsT=nT[:, c, blk * 128:(blk + 1) * 128],
                                rhs=wg[s_][:, c, 256:384], start=(c == 0), stop=(c == 7)),
                                r=[wgb[s_], nTb[t][c]], w=[bankb[bi]])
                    evac(t % 2 == 1, V[s_][:, t * 4:(t + 1) * 4, :], banks[bi][:, :].rearrange("p (a b) -> p a b", a=4),
                         1.0, [bankb[bi]], [vb_[s_][t]])
                return c_v
            for t in range(NT):
                chunks.append(mk_v(t))
            return chunks

        def project(g):
            for ch in project_chunks(g, scorerot, True):
                ch()

        b7rot = Rot([7])
        proj_pending = []

        def proj_pop(n=1):
            for _ in range(n):
                if proj_pending:
                    proj_pending.pop(0)()

        prot = Rot([0, 1, 2, 3])
        deferred = []
        b_dummy = Buf("dummy")

        def pe_keepwarm(n):
            for _ in range(n):
                sc.op("tensor", lambda e: e.matmul(banks[7][:, :], lhsT=ones[:, :], rhs=ctab[:, 0:512], start=True, stop=True),
                      r=[b_ones, b_const], w=[bankb[7]])

        def run_deferred(force=False):
            keep = []
            for item in deferred:
                item[0] -= 1
                if item[0] <= 0 or force:
                    item[1]()
                else:
                    keep.append(item)
            deferred[:] = keep

        def c0_of(t, J):
            r = J - 4 * t
            return (max(r, 0) * 128, r)

        def head_norm_finish(chunk, t, src, srcb, lhs, inv_n, gain, sq_on_act=False):
            cols = slice(t * TW, (t + 1) * TW)
            if sq_on_act:
                sc.op("scalar", lambda e: e.activation(out=sq[:, 0, :], in_=src, func=AF.Square), r=[srcb], w=[sqbs[0]])
            else:
                sc.op("vector", lambda e: e.tensor_tensor(out=sq[:, 0, :], in0=src, in1=src, op=ALU.mult), r=[srcb], w=[sqbs[0]])

            def pe_part():
                sc.op("tensor", lambda e: e.matmul(banks[7][:, :], lhsT=lhs, rhs=sq[:, 0, :], start=True, stop=True),
                      r=[sqbs[0], b_ones], w=[bankb[7]])
                ri = rsrot.next()
                sc.op("scalar", lambda e: e.activation(out=ln_t[:, :], in_=banks[7][:, :], func=AF.Ln, bias=EPS, scale=inv_n),
                      r=[bankb[7]], w=[lnb])
                sc.op("scalar", lambda e: e.activation(out=rs_t[ri][:, :], in_=ln_t[:, :], func=AF.Exp, scale=-0.5),
                      r=[lnb], w=[rsb[ri]])
                sc.op("vector", lambda e: e.scalar_tensor_tensor(out=mixT[:, chunk, cols], in0=src, scalar=gain,
                                                                 in1=rs_t[ri][:, :], op0=ALU.mult, op1=ALU.mult),
                      r=[srcb, rsb[ri], b_lw, b_gv], w=[mixb[chunk][t]])
            deferred.append([6, pe_part])

        def diff_attention(h):
            s_ = h % 2
            units = []
            for t in range(NT):
                for m in range(2):
                    nJ = 4 * t + 4
                    for J in range(nJ):
                        units.append((t, m, J, nJ))
            acc_pairs = [(3, 4), (5, 6)]
            state = {}

            def stage_qk(u):
                t, m, J, nJ = u
                c0, r = c0_of(t, J)
                bi = scorerot.next()
                pi = prot.next()
                state[u] = (bi, pi)
                qs = slice(t * TW + c0, (t + 1) * TW)
                ks = slice(J * 128, (J + 1) * 128)
                if m == 0:
                    lhsT, rhs = KA[s_][:, ks], QA[s_][:, qs]
                else:
                    lhsT, rhs = KB[s_][:, ks], QB[s_][:, qs]
                rdb = [kb_[s_][J // 4], qb_[s_][t]] if m == 0 else [kB_[s_][J // 4], qB_[s_][t]]
                sc.op("tensor", lambda e: e.matmul(banks[bi][:, c0:TW], lhsT=lhsT, rhs=rhs, start=True, stop=(r < 0)),
                      r=rdb, w=[bankb[bi]])
                if r >= 0:
                    sc.op("tensor", lambda e: e.matmul(banks[bi][:, c0:c0 + 128], lhsT=ident, rhs=dtabs[h], start=False, stop=True),
                          r=[b_const], w=[bankb[bi]])
                sc.op("scalar", lambda e: e.activation(out=P[pi][:, c0:TW], in_=banks[bi][:, c0:TW], func=AF.Exp),
                      r=[bankb[bi]], w=[Pb[pi]])

            def stage_av(u):
                t, m, J, nJ = u
                c0, r = c0_of(t, J)
                bi, pi = state.pop(u)
                po, pl = acc_pairs[(t * 2 + m) % 2]
                sc.op("tensor", lambda e: e.matmul(banks[po][:, c0:TW], lhsT=V[s_][:, J, :], rhs=P[pi][:, c0:TW],
                                                   start=(J == 0), stop=(J == nJ - 1)),
                      r=[vb_[s_][J // 4], Pb[pi]], w=[bankb[po]])
                sc.op("tensor", lambda e: e.matmul(banks[pl][:, c0:TW], lhsT=ones[:, :], rhs=P[pi][:, c0:TW],
                                                   start=(J == 0), stop=(J == nJ - 1)),
                      r=[b_ones, Pb[pi]], w=[bankb[pl]])
                if J == nJ - 1:
                    am = af[m]
                    sc.op("scalar", lambda e: e.activation(out=rl, in_=banks[pl][:, :], func=AF.Ln), r=[bankb[pl]], w=[rlb])
                    sc.op("scalar", lambda e: e.activation(out=rl, in_=rl, func=AF.Exp, scale=-1.0), r=[rlb], w=[rlb])
                    sc.op("vector", lambda e: e.tensor_tensor(out=am, in0=banks[po][:, :], in1=rl, op=ALU.mult),
                          r=[bankb[po], rlb], w=[afb[m]])
                    if m == 1:
                        sc.op("vector", lambda e: e.scalar_tensor_tensor(out=af[2], in0=af[1], scalar=neglam, in1=af[0],
                                                                         op0=ALU.mult, op1=ALU.add),
                              r=[afb[0], afb[1], b_lw], w=[afb[2]])
                        head_norm_finish(h, t, af[2], afb[2], ones[:, :], 1.0 / 128.0, gsub)

            SK = 3
            n = len(units)
            proj_pending.extend(project_chunks(h + 1, b7rot, False))
            for i in range(n + SK):
                sc.group_begin("tensor")
                if i < n:
                    stage_qk(units[i])
                if i % 5 == 1:
                    proj_pop()
                if i - SK >= 0:
                    stage_av(units[i - SK])
                pe_keepwarm(DUMMY_DIFF)
                run_deferred()
                sc.group_end("tensor")
            proj_pop(100)

        sbscore = Rot([0, 1, 2, 3, 6])
        arot = Rot([0, 1, 2, 3, 4])
        erot = Rot([0, 1])

        def sb_items(p):
            order = range(NT) if p % 2 == 0 else reversed(range(NT))
            return [(p, t, hh) for t in order for hh in range(2)]

        sb_state = {"it": 0}

        def sb_pass1_units(item, idx):
            p, t, hh = item
            nJ = 4 * t + 4
            return [("p1", item, idx, J, nJ) for J in range(nJ)]

        def sb_pass2_units(item, idx):
            p, t, hh = item
            nJ = 4 * t + 4
            return [("p2", item, idx, J, nJ) for J in range(nJ)]

        drot = Rot([0, 1, 2])

        def sb_front(u):
            kind, (p, t, hh), idx, J, nJ = u
            s_ = p % 2
            b0 = 64 * hh
            c0, r = c0_of(t, J)
            qs = slice(t * TW + c0, (t + 1) * TW)
            ks = slice(J * 128, (J + 1) * 128)
            bi = sbscore.next()
            spi = idx % 2
            if kind == "p1":
                Qh = (QA if hh == 0 else QB)[s_]
                qhb = (qb_ if hh == 0 else qB_)[s_][t]
                mm = [(KA[s_][:, ks], Qh[:, qs], slice(c0, TW), [kb_[s_][J // 4], qhb])]
                if r >= 0:
                    mm.append((ident, mtab, slice(c0, c0 + 128), [b_const]))
            else:
                mm = [(negtri, SP[spi][:, J, c0:TW], slice(c0, TW), [b_const, SPb[spi][J]])]
                if J < nJ - 1:
                    mm.append((nu[:, J * 128:(J + 1) * 128], CS[spi][:, c0:TW], slice(c0, TW), [b_const, CSb[spi]]))
            ndup = DUMMY_SB if kind == "p1" else max(DUMMY_SB - 1, 0)
            for _ in range(ndup):
                lhsT, rhs, osl, rb = mm[0]
                sc.op("tensor", lambda e, lhsT=lhsT, rhs=rhs, osl=osl: e.matmul(
                    banks[bi][:, osl], lhsT=lhsT, rhs=rhs, start=True, stop=True), r=rb, w=[bankb[bi]])
            for i, (lhsT, rhs, osl, rb) in enumerate(mm):
                sc.op("tensor", lambda e, lhsT=lhsT, rhs=rhs, osl=osl, i=i: e.matmul(
                    banks[bi][:, osl], lhsT=lhsT, rhs=rhs, start=(i == 0), stop=(i == len(mm) - 1)),
                    r=rb, w=[bankb[bi]])
            if kind == "p1":
                sc.op("scalar", lambda e: e.activation(out=EB[spi][:, J, c0:TW], in_=banks[bi][:, c0:TW], func=AF.Exp),
                      r=[bankb[bi]], w=[EBb[spi][J]])

                def ln_part():
                    sc.op("scalar", lambda e: e.activation(out=SP[spi][:, J, c0:TW], in_=EB[spi][:, J, c0:TW], func=AF.Ln, bias=1.0),
                          r=[EBb[spi][J]], w=[SPb[spi][J]])
                return ln_part
            else:
                di = drot.next()
                ai = arot.next()
                sc.op("scalar", lambda e: e.activation(out=Dt[di][:, c0:TW], in_=banks[bi][:, c0:TW], func=AF.Exp),
                      r=[bankb[bi]], w=[Dtb[di]])
                sc.op("vector", lambda e: e.tensor_tensor(out=A[ai][:, c0:TW], in0=EB[spi][:, J, c0:TW], in1=Dt[di][:, c0:TW], op=ALU.mult),
                      r=[EBb[spi][J], Dtb[di]], w=[Ab[ai]])
                return ai

        def sb_back(u, ai):
            kind, (p, t, hh), idx, J, nJ = u
            s_ = p % 2
            b0 = 64 * hh
            c0, r = c0_of(t, J)
            spi = idx % 2
            if kind == "p1":
                sc.op("tensor", lambda e: e.matmul(banks[4][:, c0:TW], lhsT=esel[:, J * 128:(J + 1) * 128],
                                                   rhs=SP[spi][:, J, c0:TW], start=(J == 0), stop=(J == nJ - 1)),
                      r=[b_const, SPb[spi][J]], w=[bankb[4]])
                if J == nJ - 1:
                    sc.op("vector", lambda e: e.tensor_copy(out=CS[spi][0:16, :], in_=banks[4][0:16, :]),
                          r=[bankb[4]], w=[CSb[spi]])
            else:
                po = 5
                oi = 2 + (t % 2)
                sc.op("tensor", lambda e: e.matmul(banks[po][:, c0:TW], lhsT=V[s_][:, J, :],
                                                   rhs=A[ai][:, c0:TW], start=(J == 0), stop=(J == nJ - 1)),
                      r=[vb_[s_][J // 4], Ab[ai]], w=[bankb[po]])
                if J == nJ - 1:
                    sc.op("vector", lambda e: e.tensor_copy(out=af[oi][b0:b0 + 64, :], in_=banks[po][b0:b0 + 64, :]),
                          r=[bankb[po]], w=[afb[oi]])
                    if hh == 1:
                        head_norm_finish(4 + p, t, af[oi], afb[oi], bones[:, :], 1.0 / 64.0, gsb)

        def sb_attention_all():
            items = []
            for p in range(4):
                items += sb_items(p)
            n = len(items)
            SKB = 4
            LAG = 4
            pend = []
            for k in range(n + 1):
                l1 = sb_pass1_units(items[k], k) if k < n else []
                l2 = [None] * LAG + (sb_pass2_units(items[k - 1], k - 1) if k >= 1 else [])
                if k < n and k % 8 == 0:
                    proj_pop(100)
                    if items[k][0] == 3:
                        sc.dma("gpsimd", wo.rearrange("p a b -> p (a b)"), wout_d[:, :], woslot,
                               w=[wob] + [b for l in nTb for b in l])
                if k < n and k % 8 == 2 and items[k][0] + 1 < 4:
                    proj_pending.extend(project_chunks(4 + items[k][0] + 1, b7rot, False))
                step_in_item = 0
                for u1, u2 in zip_longest(l1, l2):
                    sc.group_begin("tensor")
                    cur = []
                    ln_part = None
                    if u1 is not None:
                        ln_part = sb_front(u1)
                        cur.append((u1, None))
                    if u2 is not None:
                        ai = sb_front(u2)
                        cur.append((u2, ai))
                    if ln_part is not None:
                        ln_part()
                    step_in_item += 1
                    if step_in_item % 3 == 2:
                        proj_pop()
                    pend.append(cur)
                    if len(pend) > SKB:
                        for (u, ai) in pend.pop(0):
                            sb_back(u, ai)
                    run_deferred()
                    sc.group_end("tensor")
            while pend:
                for (u, ai) in pend.pop(0):
                    sb_back(u, ai)
            run_deferred(force=True)
            run_deferred(force=True)

        project(0)
        for g in range(4):
            diff_attention(g)
        sc.fence(("scalar", "vector"), xsb)
        sc.fence(("scalar",), [b for l in kB_ for b in l] + Pb)
        sb_attention_all()
        sc.barrier()

        if debug and stop is None:
            dslot = sc.new_slot("dbg")
            dbgv = dbg_d.rearrange("(c p) t -> p c t", p=128)
            arena.reset()
            dstage = arena.alloc([8, S], F32)
            dsb = Buf("dstage")
            for c in range(8):
                sc.op("vector", lambda e, c=c: e.tensor_copy(out=dstage[:, c, :], in_=mixT[:, c, :]),
                      r=[mixb[c][t] for t in range(NT)], w=[dsb])
            sc.dma("sync", dbgv[:, :, :], dstage[:, :, :], dslot, r=[dsb])
            sc.barrier()

        arena.reset()
        hT = arena.alloc([8, S], F32)
        hb = [[Buf(f"h{c}_{t}") for t in range(NT)] for c in range(8)]
        hslot = [sc.new_slot(f"h{t}") for t in range(NT)]
        FG = 8
        wup = [arena.alloc([8, 512], BF16) for _ in range(2)]
        wdn = [arena.alloc([4, 1024], BF16) for _ in range(2)]
        wupb = [Buf("wup0"), Buf("wup1")]
        wdnb = [Buf("wdn0"), Buf("wdn1")]
        wupslot = [sc.new_slot("wup0"), sc.new_slot("wup1")]
        wdnslot = [sc.new_slot("wdn0"), sc.new_slot("wdn1")]
        uT = [arena.alloc([4, TW], BF16) for _ in range(2)]
        ub = [[Buf(f"u{i}_{f}") for f in range(4)] for i in range(2)]
        rr = [arena.alloc([1, TW], F32)[:, 0, :] for _ in range(2)]
        rrb = [Buf("rr0"), Buf("rr1")]

        for t in range(NT):
            cols = slice(t * TW, (t + 1) * TW)
            sc.dma("sync", hT[:, :, cols], xTv[:, :, cols], hslot[t], w=[hb[c][t] for c in range(8)])

        def load_ffn_group(G):
            k = G % 2
            sc.dma("gpsimd", wup[k].rearrange("p a b -> p (a b)"), wup_d[G, :, :], wupslot[k], w=[wupb[k]])
            sc.dma("gpsimd", wdn[k].rearrange("p a b -> p (a b)"), wdn_d[G, :, :], wdnslot[k], w=[wdnb[k]])

        load_ffn_group(0)
        load_ffn_group(1)

        brot = Rot([0, 1, 2, 3, 4, 5, 6])
        for t in range(NT):
            cols = slice(t * TW, (t + 1) * TW)
            for dc in range(8):
                bi = brot.next()
                for ec in range(8):
                    sc.op("tensor", lambda e, bi=bi, ec=ec, dc=dc, cols=cols: e.matmul(
                        banks[bi][:, :], lhsT=wo[:, ec, dc * 128:(dc + 1) * 128], rhs=mixT[:, ec, cols],
                        start=(ec == 0), stop=(ec == 7)), r=[wob, mixb[ec][t]], w=[bankb[bi]])
                sc.op("vector", lambda e, bi=bi, dc=dc, cols=cols: e.tensor_tensor(
                    out=hT[:, dc, cols], in0=banks[bi][:, :], in1=hT[:, dc, cols], op=ALU.add),
                    r=[bankb[bi]], w=[hb[dc][t]])

        def dump_direct(src3d, bufs):
            dslot2 = sc.new_slot("dbg2")
            dbgv2 = dbg_d.rearrange("(c p) t -> p c t", p=128)
            sc.dma("sync", dbgv2[:, :, :], src3d[:, :, :], dslot2, r=bufs)
            sc.barrier()
            block = es.enter_context(nc.Block())
            sc.emit(block)
            return nc

        if stop == "C":
            return dump_direct(hT, [b for l in hb for b in l])

        sc.fence(("vector",), [wob])
        for t in range(NT):
            cols = slice(t * TW, (t + 1) * TW)
            rs, rb = rms_stats(lambda c, cols=cols: hT[:, c, cols], [hb[c][t] for c in range(8)], 8, 7, 1.0 / D, ones[:, :])
            for c in range(8):
                sc.op("vector", lambda e, c=c, rs=rs, cols=cols: e.scalar_tensor_tensor(
                    out=nT[:, c, cols], in0=hT[:, c, cols], scalar=g2(c), in1=rs[:, :], op0=ALU.mult, op1=ALU.mult),
                    r=[hb[c][t], rb, b_gv], w=[nTb[t][c]])

        sc.fence(("vector",), [b for l in mixb for b in l])
        ost = [mixT[:, 2 * i:2 * i + 2, :].rearrange("p a b -> p (a b)").bitcast(F32).rearrange("p (a b) -> p a b", a=4)
               for i in range(4)]
        ostb = [[Buf(f"ost{i}_{j}") for j in range(2)] for i in range(4)]
        oslot = [[sc.new_slot(f"o{i}_{j}") for j in range(2)] for i in range(4)]
        orot = Rot([0, 1, 2, 3])
        def final_tile(t):
            cols = slice(t * TW, (t + 1) * TW)
            rs, rb = rms_stats(lambda c, cols=cols: hT[:, c, cols], [hb[c][t] for c in range(8)], 8, 7, 1.0 / D, ones[:, :])
            for half in range(2):
                oi = orot.next()
                for cc in range(4):
                    c = half * 4 + cc
                    q = cc // 2
                    sc.op("vector", lambda e, c=c, cc=cc, oi=oi, rs=rs, cols=cols: e.scalar_tensor_tensor(
                        out=ost[oi][:, cc, :], in0=hT[:, c, cols], scalar=gf(c), in1=rs[:, :], op0=ALU.mult, op1=ALU.mult),
                        r=[hb[c][t], rb, b_gv], w=[ostb[oi][q]])
                    if cc % 2 == 1:
                        sc.dma("sync", outTv[:, half * 4 + 2 * q:half * 4 + 2 * q + 2, cols], ost[oi][:, 2 * q:2 * q + 2, :],
                               oslot[oi][q], r=[ostb[oi][q]])

        urot = Rot([0, 1])
        rrot = Rot([0, 1])
        def ffn_up(G, t, ui):
            k = G % 2
            cols = slice(t * TW, (t + 1) * TW)
            for fc in range(4):
                bi = brot.next()
                for dc in range(8):
                    sc.op("tensor", lambda e, bi=bi, fc=fc, dc=dc: e.matmul(
                        banks[bi][:, :], lhsT=wup[k][:, dc, fc * 128:(fc + 1) * 128], rhs=nT[:, dc, cols],
                        start=(dc == 0), stop=(dc == 7)), r=[wupb[k], nTb[t][dc]], w=[bankb[bi]])
                ri = rrot.next()
                sc.op("scalar", lambda e, bi=bi, ri=ri: e.activation(out=rr[ri], in_=banks[bi][:, :], func=AF.Relu),
                      r=[bankb[bi]], w=[rrb[ri]])
                sc.op("vector", lambda e, ri=ri, fc=fc: e.tensor_tensor(
                    out=uT[ui][:, fc, :], in0=rr[ri], in1=rr[ri], op=ALU.mult), r=[rrb[ri]], w=[ub[ui][fc]])

        def ffn_down(G, t, ui):
            k = G % 2
            cols = slice(t * TW, (t + 1) * TW)
            for dc in range(8):
                bi = brot.next()
                for fc in range(4):
                    sc.op("tensor", lambda e, bi=bi, fc=fc, dc=dc: e.matmul(
                        banks[bi][:, :], lhsT=wdn[k][:, fc, dc * 128:(dc + 1) * 128], rhs=uT[ui][:, fc, :],
                        start=(fc == 0), stop=(fc == 3)), r=[wdnb[k], ub[ui][fc]], w=[bankb[bi]])
                sc.op("vector", lambda e, bi=bi, dc=dc: e.tensor_tensor(
                    out=hT[:, dc, cols], in0=banks[bi][:, :], in1=hT[:, dc, cols], op=ALU.add),
                    r=[bankb[bi]], w=[hb[dc][t]])

        seq = [(G, t) for G in range(FG) for t in range(NT)]
        uis = [i % 2 for i in range(len(seq))]
        for i in range(len(seq) + 1):
            if i < len(seq):
                ffn_up(seq[i][0], seq[i][1], uis[i])
            if i >= 1:
                G, t = seq[i - 1]
                ffn_down(G, t, uis[i - 1])
                if G == FG - 1 and t >= 1:
                    final_tile(t - 1)
                if t == NT - 1 and G + 2 < FG:
                    load_ffn_group(G + 2)

        final_tile(NT - 1)

        sc.barrier()

        block = es.enter_context(nc.Block())
        sc.emit(block)
    return nc


def _const_tables():
    ct = np.zeros((128, NCT), np.float32)
    ct[:, 0:128] = np.eye(128, dtype=np.float32)
    j = np.arange(128)[:, None]
    s = np.arange(128)[None, :]
    ct[:, 128:256] = np.where(j >= s, -1.0, 0.0)
    ss_ = np.arange(128)[:, None]
    tt = np.arange(128)[None, :]
    ct[:, 256:384] = np.where(ss_ < tt, 0.0, NEG)
    for h in range(4):
        sl = SLOPES[h]
        d = np.where(ss_ <= tt, 0.0, np.where((ss_ // 64) <= (tt // 64), -2.0 * sl * (ss_ - tt), NEG))
        ct[:, 384 + 128 * h:512 + 128 * h] = d
    for Jp in range(16):
        ct[:, 896 + Jp * 128 + Jp] = 1.0
    nu = np.zeros((128, NB, 128), np.float32)
    for Jp in range(16):
        for J in range(16):
            if Jp > J:
                nu[Jp, J, :] = -1.0
    t = np.arange(S)
    qaug = np.stack([-(t % 128), -128.0 * (t // 128), np.ones(S), np.ones(S)]).astype(np.float32)
    kaug = np.zeros((4, 4, S), np.float32)
    for h in range(4):
        sl = SLOPES[h]
        kaug[h] = np.stack([sl * np.ones(S), sl * np.ones(S), sl * (t % 128), sl * 128.0 * (t // 128)])
    return ct, nu.reshape(128, NB * 128), qaug, kaug


def _prep_shared(norm1_g, w_in, lambda_q1, lambda_k1, lambda_q2, lambda_k2, diff_subln_g, sb_norm_g,
                 w_out, norm2_g, w_up, w_down, final_norm_g):
    f = np.float32
    w_in = np.asarray(w_in, f)[0]
    groups = []
    for g in range(8):
        if g < 4:
            qc, kc, vc = g * 128, 512 + g * 128, 1024 + g * 128
        else:
            p = g - 4
            qc, kc, vc = 1536 + p * 128, 2048 + p * 128, 2560 + p * 128
        wgm = np.concatenate([w_in[:, qc:qc + 128], w_in[:, kc:kc + 128], w_in[:, vc:vc + 128]], axis=1)
        groups.append(wgm.reshape(8, 128, 384).transpose(1, 0, 2).reshape(128, 8 * 384))
    win = np.ascontiguousarray(np.stack(groups))
    wout = np.ascontiguousarray(np.asarray(w_out, f)[0].reshape(8, 128, 1024).transpose(1, 0, 2).reshape(128, 8 * 1024))
    wu = np.asarray(w_up, f)[0]
    wup = np.ascontiguousarray(np.stack([
        wu[:, G * 512:(G + 1) * 512].reshape(8, 128, 512).transpose(1, 0, 2).reshape(128, 8 * 512) for G in range(8)]))
    wd = np.asarray(w_down, f)[0]
    wdn = np.ascontiguousarray(np.stack([
        wd[G * 512:(G + 1) * 512, :].reshape(4, 128, 1024).transpose(1, 0, 2).reshape(128, 4 * 1024) for G in range(8)]))
    gv = np.zeros((128, 26), f)
    gv[:, 0:8] = np.asarray(norm1_g, f)[0].reshape(8, 128).T
    gv[:, 8:16] = np.asarray(norm2_g, f)[0].reshape(8, 128).T
    gv[:, 16:24] = np.asarray(final_norm_g, f).reshape(8, 128).T
    gv[:, 24] = np.asarray(diff_subln_g, f)[0]
    gv[:, 25] = np.concatenate([np.asarray(sb_norm_g, f)[0]] * 2)
    lamv = np.concatenate([np.asarray(a, f)[0] for a in (lambda_q1, lambda_k1, lambda_q2, lambda_k2)])
    lamv = np.ascontiguousarray(np.broadcast_to(lamv[None, :], (128, 256)))
    ct, nu, qaug, kaug = _const_tables()
    return {"win": win, "wout": wout, "wup": wup, "wdn": wdn, "gv": gv, "lamv": lamv,
            "ctab": ct, "nutab": nu, "qaug": qaug, "kaug": kaug}


_CACHE = {}


def kernel(x, norm1_g, w_in, lambda_q1, lambda_k1, lambda_q2, lambda_k2, diff_subln_g, sb_norm_g,
           w_out, norm2_g, w_up, w_down, final_norm_g):
    debug = bool(int(os.environ.get("MK_DEBUG", "0")))
    x = np.asarray(x, np.float32)
    shared = _prep_shared(norm1_g, w_in, lambda_q1, lambda_k1, lambda_q2, lambda_k2, diff_subln_g, sb_norm_g,
                          w_out, norm2_g, w_up, w_down, final_norm_g)
    n = x.shape[0]
    in_maps = []
    for b in range(n):
        m = dict(shared)
        m["xT"] = np.ascontiguousarray(x[b].T)
        in_maps.append(m)
    key = ("nc", debug)
    if key not in _CACHE:
        _CACHE[key] = build_program(debug=debug)
    nc = _CACHE[key]
    res = run_bass_kernel_spmd(nc, in_maps, core_ids=list(range(n)))
    out = np.stack([np.ascontiguousarray(r["outT"].T) for r in res.results]).astype(np.float32)
    if debug:
        kernel.dbg = np.stack([r["dbg"] for r in res.results])
    return out
```

```python
import os
from contextlib import ExitStack
from itertools import zip_longest

import numpy as np
import concourse.bass as bass
import concourse.mybir as mybir
from concourse.bass_utils import run_bass_kernel_spmd

F32 = mybir.dt.float32
BF16 = mybir.dt.bfloat16
AF = mybir.ActivationFunctionType
ALU = mybir.AluOpType
AX = mybir.AxisListType

D = 1024
S = 2048
NT = 4
TW = 512
NB = 16
EPS = 1e-6
NEG = -30000.0
SLOPES = [2.0 ** (-8.0 * (h + 1) / 4.0) for h in range(4)]
LAMBDA_INIT = 0.8 - 0.6

ENGINES = ("tensor", "scalar", "vector", "gpsimd", "sync")
SAME_ENGINE_SYNC = True
DUMMY_SB = int(os.environ.get('MK_DUMMY_SB', '0'))
DUMMY_DIFF = int(os.environ.get('MK_DUMMY_DIFF', '0'))


class Buf:
    __slots__ = ("name", "w", "r")

    def __init__(self, name=""):
        self.name = name
        self.w = None
        self.r = []


class Op:
    __slots__ = ("eng", "fn", "deps", "signal", "sigval", "dma_slot", "dma_val", "hoist", "seq")

    def __init__(self, eng, fn):
        self.eng = eng
        self.fn = fn
        self.deps = []
        self.signal = False
        self.sigval = None
        self.dma_slot = None
        self.dma_val = None
        self.hoist = False
        self.seq = -1


class DmaSlot:
    def __init__(self, sem):
        self.sem = sem
        self.count = 0


class Sched:
    def __init__(self, nc, es):
        self.nc = nc
        self.es = es
        self.ops = {e: [] for e in ENGINES}
        self.sems = {e: es.enter_context(nc.semaphore("s_" + e)) for e in ENGINES}
        self.slots = []

    def new_slot(self, name):
        s = DmaSlot(self.es.enter_context(self.nc.semaphore("d_" + name)))
        self.slots.append(s)
        return s

    def _add_dep(self, op, dep):
        if dep is None or dep is op:
            return
        if dep.fn is None and dep.dma_slot is None:
            return
        if dep.dma_slot is None:
            if dep.eng == op.eng:
                if op.eng == "tensor" or not SAME_ENGINE_SYNC:
                    return
            dep.signal = True
        op.deps.append(dep)

    def op(self, eng, fn, r=(), w=()):
        o = Op(eng, fn)
        self.seqc = getattr(self, "seqc", 0) + 1
        o.seq = self.seqc
        for b in r:
            self._add_dep(o, b.w)
        for b in w:
            self._add_dep(o, b.w)
            for rd in b.r:
                self._add_dep(o, rd)
        for b in w:
            b.w = o
            b.r = []
        for b in r:
            if b not in w:
                b.r.append(o)
        self.ops[eng].append(o)
        return o

    def group_begin(self, eng):
        self._gstart = (eng, len(self.ops[eng]))
        self._gseq = getattr(self, "seqc", 0)

    def group_end(self, eng):
        e, start = self._gstart
        assert e == eng
        grp = self.ops[eng][start:]
        if len(grp) < 2:
            return
        o = Op(eng, None)
        o.hoist = True
        for g in grp:
            o.deps.extend(d for d in g.deps if 0 <= d.seq <= self._gseq)
        self.ops[eng].insert(start, o)

    def fence(self, engs, bufs):
        for e in engs:
            o = Op(e, None)
            for b in bufs:
                self._add_dep(o, b.w)
                for rd in b.r:
                    self._add_dep(o, rd)
            self.ops[e].append(o)

    def dma(self, eng, out, in_, slot, r=(), w=()):
        o = self.op(eng, lambda e: e.dma_start(out=out, in_=in_), r=r, w=w)
        slot.count += 16
        o.dma_slot = slot
        o.dma_val = slot.count
        return o

    def barrier(self):
        lasts = []
        for e in ENGINES:
            real = [o for o in self.ops[e] if o.fn is not None]
            if real:
                lasts.append(real[-1])
        for e in ENGINES:
            o = Op(e, None)
            for l in lasts:
                if l.eng != e:
                    if l.dma_slot is None:
                        l.signal = True
                    o.deps.append(l)
            for sl in self.slots:
                if sl.count:
                    d = Op("dma", None)
                    d.dma_slot = sl
                    d.dma_val = sl.count
                    o.deps.append(d)
            self.ops[e].append(o)

    def emit(self, block):
        for e in ENGINES:
            c = 0
            for o in self.ops[e]:
                if o.signal and o.fn is not None:
                    c += 1
                    o.sigval = c
        sems = self.sems

        def run(engname):
            def body(eng):
                waited = {}
                for o in self.ops[engname]:
                    need = {}
                    for d in o.deps:
                        if d.dma_slot is not None:
                            sem, val = d.dma_slot.sem, d.dma_val
                        else:
                            sem, val = sems[d.eng], d.sigval
                        key = id(sem)
                        if key not in need or need[key][1] < val:
                            need[key] = (sem, val)
                    for key, (sem, val) in need.items():
                        if waited.get(key, 0) < val:
                            eng.wait_ge(sem, val)
                            waited[key] = val
                    if o.fn is None:
                        continue
                    ins = o.fn(eng)
                    if o.dma_slot is not None:
                        ins.then_inc(o.dma_slot.sem, 16)
                    elif o.signal:
                        ins.then_inc(sems[engname], 1)
            return body

        block.tensor(run("tensor"))
        block.scalar(run("scalar"))
        block.vector(run("vector"))
        block.gpsimd(run("gpsimd"))
        block.sync(run("sync"))


class Arena:
    def __init__(self, ap, words):
        self.ap = ap
        self.words = words
        self.off = 0

    def reset(self):
        self.off = 0

    def alloc(self, shape, dtype):
        n = int(np.prod(shape))
        words = n if dtype == F32 else (n + 1) // 2
        assert self.off + words <= self.words, (self.off, words, self.words)
        a = self.ap[:, self.off:self.off + words]
        self.off += words
        if dtype != F32:
            a = a.bitcast(dtype)
        if len(shape) == 2:
            a = a.rearrange("p (a b) -> p a b", a=shape[0])
        elif len(shape) == 3:
            a = a.rearrange("p (a b c) -> p a b c", a=shape[0], b=shape[1])
        return a


class Rot:
    def __init__(self, items):
        self.items = list(items)
        self.i = 0

    def next(self):
        x = self.items[self.i % len(self.items)]
        self.i += 1
        return x


NCT = 7 * 128 + 16 * 128
ARENA_WORDS = 116 * 256


def build_program(debug=False, stop=None):
    nc = bass.Bass("TRN2", target_bir_lowering=False)
    dt = nc.dram_tensor
    xT_d = dt("xT", [D, S], F32, kind="ExternalInput").ap()
    win_d = dt("win", [8, 128, 8 * 384], F32, kind="ExternalInput").ap()
    wout_d = dt("wout", [128, 8 * 1024], F32, kind="ExternalInput").ap()
    wup_d = dt("wup", [8, 128, 8 * 512], F32, kind="ExternalInput").ap()
    wdn_d = dt("wdn", [8, 128, 4 * 1024], F32, kind="ExternalInput").ap()
    gv_d = dt("gv", [128, 26], F32, kind="ExternalInput").ap()
    lam_d = dt("lamv", [128, 256], F32, kind="ExternalInput").ap()
    ctab_d = dt("ctab", [128, NCT], F32, kind="ExternalInput").ap()
    nu_d = dt("nutab", [128, NB * 128], F32, kind="ExternalInput").ap()
    qaug_d = dt("qaug", [4, S], F32, kind="ExternalInput").ap()
    kaug_d = dt("kaug", [4, 4, S], F32, kind="ExternalInput").ap()
    outT_d = dt("outT", [D, S], F32, kind="ExternalOutput").ap()
    dbg_d = None
    if debug:
        dbg_d = dt("dbg", [D, S], F32, kind="ExternalOutput").ap()

    xTv = xT_d.rearrange("(c p) t -> p c t", p=128)
    outTv = outT_d.rearrange("(c p) t -> p c t", p=128)

    with ExitStack() as es:
        sb = lambda name, shape, dtype: es.enter_context(nc.sbuf_tensor(name, shape, dtype))
        nT = sb("nT", [128, 8, S], BF16)
        mixT = sb("mixT", [128, 8, S], BF16)
        sq = sb("sq", [128, 8, TW], BF16)
        rs_t = [sb(f"rs{i}", [128, TW], F32) for i in range(2)]
        ln_t = sb("lnv", [128, TW], F32)
        ctab = sb("ctab_sb", [128, NCT], BF16)
        ones = sb("ones", [128, 128], BF16)
        bones = sb("bones", [128, 128], BF16)
        nu = sb("nu", [128, NB * 128], BF16)
        gv = sb("gv_sb", [128, 26], F32)
        lamt = sb("lamt", [128, 256], F32)
        lamw = sb("lamw", [128, 128], F32)
        lams = sb("lams", [128, 8], F32)
        arena_t = sb("arena", [128, ARENA_WORDS], F32)
        arena = Arena(arena_t[:, :], ARENA_WORDS)
        banks = [es.enter_context(nc.psum_tensor(f"pb{i}", [128, TW], F32)) for i in range(8)]
        bankb = [Buf(f"bank{i}") for i in range(8)]

        sc = Sched(nc, es)
        ident = ctab[:, 0:128]
        negtri = ctab[:, 128:256]
        mtab = ctab[:, 256:384]
        dtabs = [ctab[:, 384 + 128 * h: 512 + 128 * h] for h in range(4)]
        esel = ctab[:, 896:896 + 16 * 128]
        g1 = lambda c: gv[:, c:c + 1]
        g2 = lambda c: gv[:, 8 + c:9 + c]
        gf = lambda c: gv[:, 16 + c:17 + c]
        gsub = lams[:, 4:5]
        gsb = gv[:, 25:26]
        neglam = lams[:, 3:4]

        b_const = Buf("const")
        b_gv = Buf("gv")
        b_lam = Buf("lam")
        s_c = [sc.new_slot(f"c{i}") for i in range(4)]
        sc.dma("gpsimd", ctab[:, :], ctab_d[:, :], s_c[0], w=[b_const])
        sc.dma("gpsimd", nu[:, :], nu_d[:, :], s_c[1], w=[b_const])
        sc.dma("sync", gv[:, :], gv_d[:, :], s_c[2], w=[b_gv])
        sc.dma("sync", lamt[:, :], lam_d[:, :], s_c[3], w=[b_lam])
        b_ones = Buf("ones")
        sc.op("gpsimd", lambda e: e.memset(ones[:, :], 1.0), w=[b_ones])
        sc.op("gpsimd", lambda e: e.memset(bones[:, :], 0.0), w=[b_ones])
        sc.op("gpsimd", lambda e: e.memset(bones[0:64, 0:64], 1.0), w=[b_ones])
        sc.op("gpsimd", lambda e: e.memset(bones[64:128, 64:128], 1.0), w=[b_ones])
        b_lw = Buf("lamw")
        sc.op("vector", lambda e: e.tensor_tensor(out=lamw[:, 0:64], in0=lamt[:, 0:64], in1=lamt[:, 64:128], op=ALU.mult), r=[b_lam], w=[b_lw])
        sc.op("vector", lambda e: e.tensor_tensor(out=lamw[:, 64:128], in0=lamt[:, 128:192], in1=lamt[:, 192:256], op=ALU.mult), r=[b_lam], w=[b_lw])
        sc.op("vector", lambda e: e.tensor_reduce(out=lams[:, 0:1], in_=lamw[:, 0:64], axis=AX.X, op=ALU.add), r=[b_lw], w=[b_lw])
        sc.op("vector", lambda e: e.tensor_reduce(out=lams[:, 1:2], in_=lamw[:, 64:128], axis=AX.X, op=ALU.add), r=[b_lw], w=[b_lw])
        sc.op("scalar", lambda e: e.activation(out=lams[:, 0:2], in_=lams[:, 0:2], func=AF.Exp), r=[b_lw], w=[b_lw])
        sc.op("vector", lambda e: e.tensor_tensor(out=lams[:, 2:3], in0=lams[:, 1:2], in1=lams[:, 0:1], op=ALU.subtract), r=[b_lw], w=[b_lw])
        sc.op("vector", lambda e: e.tensor_scalar(out=lams[:, 3:4], in0=lams[:, 2:3], scalar1=-LAMBDA_INIT, scalar2=None, op0=ALU.add), r=[b_lw], w=[b_lw])
        sc.op("vector", lambda e: e.tensor_scalar(out=lams[:, 4:5], in0=gv[:, 24:25], scalar1=1.0 - LAMBDA_INIT, scalar2=None, op0=ALU.mult), r=[b_gv, b_lw], w=[b_lw])

        nTb = [[Buf(f"nT{t}_{c}") for c in range(8)] for t in range(NT)]
        mixb = [[Buf(f"mix{c}_{t}") for t in range(NT)] for c in range(8)]
        sqbs = [Buf(f"sq{c}") for c in range(8)]
        rsb = [Buf("rs0"), Buf("rs1")]
        lnb = Buf("lnv")
        rsrot = Rot([0, 1])

        def rms_stats(src_c, src_bufs, nch, bank_i, inv_n, lhs_ones):
            for c in range(nch):
                sc.op("scalar", lambda e, c=c: e.activation(out=sq[:, c, :], in_=src_c(c), func=AF.Square),
                      r=src_bufs, w=[sqbs[c]])
            for c in range(nch):
                sc.op("tensor", lambda e, c=c: e.matmul(banks[bank_i][:, :], lhsT=lhs_ones, rhs=sq[:, c, :],
                                                        start=(c == 0), stop=(c == nch - 1)),
                      r=[sqbs[c], b_ones], w=[bankb[bank_i]])
            ri = rsrot.next()
            sc.op("scalar", lambda e: e.activation(out=ln_t[:, :], in_=banks[bank_i][:, :], func=AF.Ln, bias=EPS, scale=inv_n),
                  r=[bankb[bank_i]], w=[lnb])
            sc.op("scalar", lambda e: e.activation(out=rs_t[ri][:, :], in_=ln_t[:, :], func=AF.Exp, scale=-0.5),
                  r=[lnb], w=[rsb[ri]])
            return rs_t[ri], rsb[ri]

        arena.reset()
        CS_ = [arena.alloc([1, TW], BF16) for _ in range(2)]
        rl = arena.alloc([1, TW], F32)[:, 0, :]
        rlb = Buf("rl")
        wg1 = arena.alloc([8, 384], BF16)
        wg = [wg1, wg1]
        wgb1 = Buf("wg")
        wgb = [wgb1, wgb1]
        wgs1 = sc.new_slot("wg")
        wgslot = [wgs1, wgs1]
        V = [arena.alloc([NB, 128], BF16) for _ in range(2)]
        QA = [arena.alloc([1, S], BF16) for _ in range(2)]
        KA = [arena.alloc([1, S], BF16) for _ in range(2)]
        QB = [arena.alloc([1, S], BF16) for _ in range(2)]
        u_off = arena.off
        KB = [arena.alloc([1, S], BF16) for _ in range(2)]
        P_ = [arena.alloc([1, TW], BF16) for _ in range(4)]
        arena.alloc([4, TW], BF16)
        EB = [Arena(arena.ap[:, u_off:u_off + 4096], 4096).alloc([NB, TW], BF16), arena.alloc([NB, TW], BF16)]
        EBb = [[Buf(f"EB{i}_{j}") for j in range(NB)] for i in range(2)]
        QA = [a[:, 0, :] for a in QA]; QB = [a[:, 0, :] for a in QB]
        KA = [a[:, 0, :] for a in KA]; KB = [a[:, 0, :] for a in KB]
        CS = [a[:, 0, :] for a in CS_]
        CSb = [Buf("CS0"), Buf("CS1")]
        qb_ = [[Buf(f"q{s}_{t}") for t in range(NT)] for s in range(2)]
        kb_ = [[Buf(f"k{s}_{t}") for t in range(NT)] for s in range(2)]
        qB_ = [[Buf(f"qB{s}_{t}") for t in range(NT)] for s in range(2)]
        kB_ = [[Buf(f"kB{s}_{t}") for t in range(NT)] for s in range(2)]
        vb_ = [[Buf(f"v{s}_{t}") for t in range(NT)] for s in range(2)]
        augslot = [[sc.new_slot(f"aug{i}_{j}") for j in range(2)] for i in range(4)]
        P = [a[:, 0, :] for a in P_]
        Pb = [Buf(f"P{i}") for i in range(4)]
        SP = [arena.alloc([NB, TW], BF16) for _ in range(2)]
        SPb = [[Buf(f"SP{i}_{j}") for j in range(NB)] for i in range(2)]
        Dt = [arena.alloc([1, TW], BF16)[:, 0, :] for _ in range(3)]
        Dtb = [Buf(f"Dt{i}") for i in range(3)]
        A = [arena.alloc([1, TW], BF16)[:, 0, :] for _ in range(5)]
        Ab = [Buf(f"A{i}") for i in range(5)]
        af = [sq[:, 2 + 2 * i:4 + 2 * i, :].rearrange("p a b -> p (a b)").bitcast(F32) for i in range(3)]
        af.append(arena.alloc([1, TW], F32)[:, 0, :])
        afb = [Buf(f"af{i}") for i in range(4)]

        for s_ in range(2):
            sc.op("gpsimd", lambda e, s_=s_: e.memset(QB[s_][0:64, :], 0.0), w=qB_[s_])
            sc.op("gpsimd", lambda e, s_=s_: e.memset(KB[s_][0:64, :], 0.0), w=kB_[s_])
            sc.op("gpsimd", lambda e, s_=s_: e.memset(QA[s_][64:128, :], 0.0), w=qb_[s_])
            sc.op("gpsimd", lambda e, s_=s_: e.memset(KA[s_][64:128, :], 0.0), w=kb_[s_])
            sc.op("gpsimd", lambda e, s_=s_: e.memset(CS[s_][:, :], 0.0), w=[CSb[s_]])
            sc.dma("gpsimd", QA[s_][64:68, :], qaug_d[:, :], augslot[0][s_], w=qb_[s_])
            sc.dma("gpsimd", QB[s_][60:64, :], qaug_d[:, :], augslot[1][s_], w=qB_[s_])

        NXS = 3
        xar = Arena(arena.ap[:, ARENA_WORDS - NXS * 4096:ARENA_WORDS], NXS * 4096)
        xs = [xar.alloc([8, TW], F32) for _ in range(NXS)]
        xsb = [Buf(f"xs{i}") for i in range(NXS)]
        xslot = [sc.new_slot(f"x{i}") for i in range(NXS)]
        for t in range(NT):
            k = t % NXS
            cols = slice(t * TW, (t + 1) * TW)
            sc.dma("sync", xs[k][:, :, :], xTv[:, :, cols], xslot[k], w=[xsb[k]])
            rs, rb = rms_stats(lambda c, k=k: xs[k][:, c, :], [xsb[k]], 8, 7, 1.0 / D, ones[:, :])
            for c in range(8):
                sc.op("vector", lambda e, c=c, k=k, rs=rs, cols=cols: e.scalar_tensor_tensor(
                    out=nT[:, c, cols], in0=xs[k][:, c, :], scalar=g1(c), in1=rs[:, :], op0=ALU.mult, op1=ALU.mult),
                    r=[xsb[k], rb, b_gv], w=[nTb[t][c]])

        def dump_and_finish(src3d, bufs):
            dslot = sc.new_slot("dbg")
            dbgv = dbg_d.rearrange("(c p) t -> p c t", p=128)
            arena.reset()
            dstage = arena.alloc([8, S], F32)
            dsb = Buf("dstage")
            for c in range(8):
                sc.op("vector", lambda e, c=c: e.tensor_copy(out=dstage[:, c, :], in_=src3d[:, c, :]), r=bufs, w=[dsb])
            sc.dma("sync", dbgv[:, :, :], dstage[:, :, :], dslot, r=[dsb])
            sc.barrier()
            block = es.enter_context(nc.Block())
            sc.emit(block)
            return nc

        if stop == "A":
            return dump_and_finish(nT, [b for l in nTb for b in l])


        wo = nT[:, 0:4, :].rearrange("p a b -> p (a b)").rearrange("p (a b) -> p a b", a=8)
        wob = Buf("wo")
        woslot = sc.new_slot("wo")
        scorerot = Rot([0, 1, 2])

        def project_chunks(g, brot_, act_ok):
            s_ = g % 2
            is_diff = g < 4
            chunks = []

            def c_dma():
                sc.dma("gpsimd", wg[s_].rearrange("p a b -> p (a b)"), win_d[g, :, :], wgslot[s_], w=[wgb[s_]])
                if is_diff:
                    sc.dma("gpsimd", KA[s_][64:68, :], kaug_d[g, :, :], augslot[2][s_], w=kb_[s_])
                    sc.dma("gpsimd", KB[s_][60:64, :], kaug_d[g, :, :], augslot[3][s_], w=kB_[s_])
                elif g in (4, 5):
                    sc.op("gpsimd", lambda e: e.memset(QA[s_][64:128, :], 0.0), w=qb_[s_])
                    sc.op("gpsimd", lambda e: e.memset(QB[s_][0:64, :], 0.0), w=qB_[s_])
            chunks.append(c_dma)

            def evac(use_act, out, in_, scale, rb, wb):
                if (use_act and act_ok) or act_ok == "all":
                    sc.op("scalar", lambda e: e.activation(out=out, in_=in_, func=AF.Copy, scale=scale), r=rb, w=wb)
                else:
                    sc.op("vector", lambda e: e.tensor_scalar(out=out, in0=in_, scalar1=scale, scalar2=None, op0=ALU.mult),
                          r=rb, w=wb)

            def mk_qk(part, t):
                def c_qk():
                    bi = brot_.next()
                    cols = slice(t * TW, (t + 1) * TW)
                    for c in range(8):
                        sc.op("tensor", lambda e, c=c: e.matmul(
                            banks[bi][:, :], lhsT=wg[s_][:, c, part * 128:(part + 1) * 128], rhs=nT[:, c, cols],
                            start=(c == 0), stop=(c == 7)), r=[wgb[s_], nTb[t][c]], w=[bankb[bi]])
                    tb = (qb_ if part == 0 else kb_)[s_][t]
                    tbB = (qB_ if part == 0 else kB_)[s_][t]
                    dA = (QA if part == 0 else KA)[s_]
                    dB = (QB if part == 0 else KB)[s_]
                    scale = 0.125 if part == 0 else 1.0
                    if is_diff or part == 0:
                        evac(True, dA[0:64, cols], banks[bi][0:64, :], scale, [bankb[bi]], [tb])
                        evac(False, dB[64:128, cols], banks[bi][64:128, :], scale, [bankb[bi]], [tbB])
                    else:
                        evac(t % 2 == 0, dA[:, cols], banks[bi][:, :], scale, [bankb[bi]], [tb])
                return c_qk
            for part in range(2):
                for t in range(NT):
                    chunks.append(mk_qk(part, t))

            def mk_v(t):
                def c_v():
                    bi = brot_.next()
                    for j in range(4):
                        blk = t * 4 + j
                        for c in range(8):
                            sc.op("tensor", lambda e, c=c, j=j, blk=blk: e.matmul(
                                banks[bi][:, j * 128:(j + 1) * 128], lhsT=nT[:, c, blk * 128:(blk + 1) * 128],
                                rhs=wg[s_][:, c, 256:384], start=(c == 0), stop=(c == 7)),
                                r=[wgb[s_], nTb[t][c]], w=[bankb[bi]])
                    evac(t % 2 == 1, V[s_][:, t * 4:(t + 1) * 4, :], banks[bi][:, :].rearrange("p (a b) -> p a b", a=4),
                         1.0, [bankb[bi]], [vb_[s_][t]])
                return c_v
            for t in range(NT):
                chunks.append(mk_v(t))
            return chunks

        def project(g):
            for ch in project_chunks(g, scorerot, True):
                ch()

        b7rot = Rot([7])
        proj_pending = []

        def proj_pop(n=1):
            for _ in range(n):
                if proj_pending:
                    proj_pending.pop(0)()

        prot = Rot([0, 1, 2, 3])
        deferred = []
        b_dummy = Buf("dummy")

        def pe_keepwarm(n):
            for _ in range(n):
                sc.op("tensor", lambda e: e.matmul(banks[7][:, :], lhsT=ones[:, :], rhs=ctab[:, 0:512], start=True, stop=True),
                      r=[b_ones, b_const], w=[bankb[7]])

        def run_deferred(force=False):
            keep = []
            for item in deferred:
                item[0] -= 1
                if item[0] <= 0 or force:
                    item[1]()
                else:
                    keep.append(item)
            deferred[:] = keep

        def c0_of(t, J):
            r = J - 4 * t
            return (max(r, 0) * 128, r)

        def head_norm_finish(chunk, t, src, srcb, lhs, inv_n, gain, sq_on_act=False):
            cols = slice(t * TW, (t + 1) * TW)
            if sq_on_act:
                sc.op("scalar", lambda e: e.activation(out=sq[:, 0, :], in_=src, func=AF.Square), r=[srcb], w=[sqbs[0]])
            else:
                sc.op("vector", lambda e: e.tensor_tensor(out=sq[:, 0, :], in0=src, in1=src, op=ALU.mult), r=[srcb], w=[sqbs[0]])

            def pe_part():
                sc.op("tensor", lambda e: e.matmul(banks[7][:, :], lhsT=lhs, rhs=sq[:, 0, :], start=True, stop=True),
                      r=[sqbs[0], b_ones], w=[bankb[7]])
                ri = rsrot.next()
                sc.op("scalar", lambda e: e.activation(out=ln_t[:, :], in_=banks[7][:, :], func=AF.Ln, bias=EPS, scale=inv_n),
                      r=[bankb[7]], w=[lnb])
                sc.op("scalar", lambda e: e.activation(out=rs_t[ri][:, :], in_=ln_t[:, :], func=AF.Exp, scale=-0.5),
                      r=[lnb], w=[rsb[ri]])
                sc.op("vector", lambda e: e.scalar_tensor_tensor(out=mixT[:, chunk, cols], in0=src, scalar=gain,
                                                                 in1=rs_t[ri][:, :], op0=ALU.mult, op1=ALU.mult),
                      r=[srcb, rsb[ri], b_lw, b_gv], w=[mixb[chunk][t]])
            deferred.append([6, pe_part])

        def diff_attention(h):
            s_ = h % 2
            units = []
            for t in range(NT):
                for m in range(2):
                    nJ = 4 * t + 4
                    for J in range(nJ):
                        units.append((t, m, J, nJ))
            acc_pairs = [(3, 4), (5, 6)]
            state = {}

            def stage_qk(u):
                t, m, J, nJ = u
                c0, r = c0_of(t, J)
                bi = scorerot.next()
                pi = prot.next()
                state[u] = (bi, pi)
                qs = slice(t * TW + c0, (t + 1) * TW)
                ks = slice(J * 128, (J + 1) * 128)
                if m == 0:
                    lhsT, rhs = KA[s_][:, ks], QA[s_][:, qs]
                else:
                    lhsT, rhs = KB[s_][:, ks], QB[s_][:, qs]
                rdb = [kb_[s_][J // 4], qb_[s_][t]] if m == 0 else [kB_[s_][J // 4], qB_[s_][t]]
                sc.op("tensor", lambda e: e.matmul(banks[bi][:, c0:TW], lhsT=lhsT, rhs=rhs, start=True, stop=(r < 0)),
                      r=rdb, w=[bankb[bi]])
                if r >= 0:
                    sc.op("tensor", lambda e: e.matmul(banks[bi][:, c0:c0 + 128], lhsT=ident, rhs=dtabs[h], start=False, stop=True),
                          r=[b_const], w=[bankb[bi]])
                sc.op("scalar", lambda e: e.activation(out=P[pi][:, c0:TW], in_=banks[bi][:, c0:TW], func=AF.Exp),
                      r=[bankb[bi]], w=[Pb[pi]])

            def stage_av(u):
                t, m, J, nJ = u
                c0, r = c0_of(t, J)
                bi, pi = state.pop(u)
                po, pl = acc_pairs[(t * 2 + m) % 2]
                sc.op("tensor", lambda e: e.matmul(banks[po][:, c0:TW], lhsT=V[s_][:, J, :], rhs=P[pi][:, c0:TW],
                                                   start=(J == 0), stop=(J == nJ - 1)),
                      r=[vb_[s_][J // 4], Pb[pi]], w=[bankb[po]])
                sc.op("tensor", lambda e: e.matmul(banks[pl][:, c0:TW], lhsT=ones[:, :], rhs=P[pi][:, c0:TW],
                                                   start=(J == 0), stop=(J == nJ - 1)),
                      r=[b_ones, Pb[pi]], w=[bankb[pl]])
                if J == nJ - 1:
                    am = af[m]
                    sc.op("scalar", lambda e: e.activation(out=rl, in_=banks[pl][:, :], func=AF.Ln), r=[bankb[pl]], w=[rlb])
                    sc.op("scalar", lambda e: e.activation(out=rl, in_=rl, func=AF.Exp, scale=-1.0), r=[rlb], w=[rlb])
                    sc.op("vector", lambda e: e.tensor_tensor(out=am, in0=banks[po][:, :], in1=rl, op=ALU.mult),
                          r=[bankb[po], rlb], w=[afb[m]])
                    if m == 1:
                        sc.op("vector", lambda e: e.scalar_tensor_tensor(out=af[2], in0=af[1], scalar=neglam, in1=af[0],
                                                                         op0=ALU.mult, op1=ALU.add),
                              r=[afb[0], afb[1], b_lw], w=[afb[2]])
                        head_norm_finish(h, t, af[2], afb[2], ones[:, :], 1.0 / 128.0, gsub)

            SK = 3
            n = len(units)
            proj_pending.extend(project_chunks(h + 1, b7rot, False))
            for i in range(n + SK):
                sc.group_begin("tensor")
                if i < n:
                    stage_qk(units[i])
                if i % 5 == 1:
                    proj_pop()
                if i - SK >= 0:
                    stage_av(units[i - SK])
                pe_keepwarm(DUMMY_DIFF)
                run_deferred()
                sc.group_end("tensor")
            proj_pop(100)

        sbscore = Rot([0, 1, 2, 3, 6])
        arot = Rot([0, 1, 2, 3, 4])
        erot = Rot([0, 1])

        def sb_items(p):
            order = range(NT) if p % 2 == 0 else reversed(range(NT))
            return [(p, t, hh) for t in order for hh in range(2)]

        sb_state = {"it": 0}

        def sb_pass1_units(item, idx):
            p, t, hh = item
            nJ = 4 * t + 4
            return [("p1", item, idx, J, nJ) for J in range(nJ)]

        def sb_pass2_units(item, idx):
            p, t, hh = item
            nJ = 4 * t + 4
            return [("p2", item, idx, J, nJ) for J in range(nJ)]

        drot = Rot([0, 1, 2])

        def sb_front(u):
            kind, (p, t, hh), idx, J, nJ = u
            s_ = p % 2
            b0 = 64 * hh
            c0, r = c0_of(t, J)
            qs = slice(t * TW + c0, (t + 1) * TW)
            ks = slice(J * 128, (J + 1) * 128)
            bi = sbscore.next()
            spi = idx % 2
            if kind == "p1":
                Qh = (QA if hh == 0 else QB)[s_]
                qhb = (qb_ if hh == 0 else qB_)[s_][t]
                mm = [(KA[s_][:, ks], Qh[:, qs], slice(c0, TW), [kb_[s_][J // 4], qhb])]
                if r >= 0:
                    mm.append((ident, mtab, slice(c0, c0 + 128), [b_const]))
            else:
                mm = [(negtri, SP[spi][:, J, c0:TW], slice(c0, TW), [b_const, SPb[spi][J]])]
                if J < nJ - 1:
                    mm.append((nu[:, J * 128:(J + 1) * 128], CS[spi][:, c0:TW], slice(c0, TW), [b_const, CSb[spi]]))
            ndup = DUMMY_SB if kind == "p1" else max(DUMMY_SB - 1, 0)
            for _ in range(ndup):
                lhsT, rhs, osl, rb = mm[0]
                sc.op("tensor", lambda e, lhsT=lhsT, rhs=rhs, osl=osl: e.matmul(
                    banks[bi][:, osl], lhsT=lhsT, rhs=rhs, start=True, stop=True), r=rb, w=[bankb[bi]])
            for i, (lhsT, rhs, osl, rb) in enumerate(mm):
                sc.op("tensor", lambda e, lhsT=lhsT, rhs=rhs, osl=osl, i=i: e.matmul(
                    banks[bi][:, osl], lhsT=lhsT, rhs=rhs, start=(i == 0), stop=(i == len(mm) - 1)),
                    r=rb, w=[bankb[bi]])
            if kind == "p1":
                sc.op("scalar", lambda e: e.activation(out=EB[spi][:, J, c0:TW], in_=banks[bi][:, c0:TW], func=AF.Exp),
                      r=[bankb[bi]], w=[EBb[spi][J]])

                def ln_part():
                    sc.op("scalar", lambda e: e.activation(out=SP[spi][:, J, c0:TW], in_=EB[spi][:, J, c0:TW], func=AF.Ln, bias=1.0),
                          r=[EBb[spi][J]], w=[SPb[spi][J]])
                return ln_part
            else:
                di = drot.next()
                ai = arot.next()
                sc.op("scalar", lambda e: e.activation(out=Dt[di][:, c0:TW], in_=banks[bi][:, c0:TW], func=AF.Exp),
                      r=[bankb[bi]], w=[Dtb[di]])
                sc.op("vector", lambda e: e.tensor_tensor(out=A[ai][:, c0:TW], in0=EB[spi][:, J, c0:TW], in1=Dt[di][:, c0:TW], op=ALU.mult),
                      r=[EBb[spi][J], Dtb[di]], w=[Ab[ai]])
                return ai

        def sb_back(u, ai):
            kind, (p, t, hh), idx, J, nJ = u
            s_ = p % 2
            b0 = 64 * hh
            c0, r = c0_of(t, J)
            spi = idx % 2
            if kind == "p1":
                sc.op("tensor", lambda e: e.matmul(banks[4][:, c0:TW], lhsT=esel[:, J * 128:(J + 1) * 128],
                                                   rhs=SP[spi][:, J, c0:TW], start=(J == 0), stop=(J == nJ - 1)),
                      r=[b_const, SPb[spi][J]], w=[bankb[4]])
                if J == nJ - 1:
                    sc.op("vector", lambda e: e.tensor_copy(out=CS[spi][0:16, :], in_=banks[4][0:16, :]),
                          r=[bankb[4]], w=[CSb[spi]])
            else:
                po = 5
                oi = 2 + (t % 2)
                sc.op("tensor", lambda e: e.matmul(banks[po][:, c0:TW], lhsT=V[s_][:, J, :],
                                                   rhs=A[ai][:, c0:TW], start=(J == 0), stop=(J == nJ - 1)),
                      r=[vb_[s_][J // 4], Ab[ai]], w=[bankb[po]])
                if J == nJ - 1:
                    sc.op("vector", lambda e: e.tensor_copy(out=af[oi][b0:b0 + 64, :], in_=banks[po][b0:b0 + 64, :]),
                          r=[bankb[po]], w=[afb[oi]])
                    if hh == 1:
                        head_norm_finish(4 + p, t, af[oi], afb[oi], bones[:, :], 1.0 / 64.0, gsb)

        def sb_attention_all():
            items = []
            for p in range(4):
                items += sb_items(p)
            n = len(items)
            SKB = 4
            LAG = 4
            pend = []
            for k in range(n + 1):
                l1 = sb_pass1_units(items[k], k) if k < n else []
                l2 = [None] * LAG + (sb_pass2_units(items[k - 1], k - 1) if k >= 1 else [])
                if k < n and k % 8 == 0:
                    proj_pop(100)
                    if items[k][0] == 3:
                        sc.dma("gpsimd", wo.rearrange("p a b -> p (a b)"), wout_d[:, :], woslot,
                               w=[wob] + [b for l in nTb for b in l])
                if k < n and k % 8 == 2 and items[k][0] + 1 < 4:
                    proj_pending.extend(project_chunks(4 + items[k][0] + 1, b7rot, False))
                step_in_item = 0
                for u1, u2 in zip_longest(l1, l2):
                    sc.group_begin("tensor")
                    step_in_item += 1
                    if step_in_item % 3 == 2:
                        proj_pop()
                    cur = []
                    ln_part = None
                    if u1 is not None:
                        ln_part = sb_front(u1)
                        cur.append((u1, None))
                    if u2 is not None:
                        ai = sb_front(u2)
                        cur.append((u2, ai))
                    if ln_part is not None:
                        ln_part()
                    pend.append(cur)
                    if len(pend) > SKB:
                        for (u, ai) in pend.pop(0):
                            sb_back(u, ai)
                    run_deferred()
                    sc.group_end("tensor")
            while pend:
                for (u, ai) in pend.pop(0):
                    sb_back(u, ai)
            run_deferred(force=True)
            run_deferred(force=True)

        project(0)
        for g in range(4):
            diff_attention(g)
        sc.fence(("scalar", "vector"), xsb)
        sc.fence(("scalar",), [b for l in kB_ for b in l] + Pb)
        sb_attention_all()
        sc.barrier()

        if debug and stop is None:
            dslot = sc.new_slot("dbg")
            dbgv = dbg_d.rearrange("(c p) t -> p c t", p=128)
            arena.reset()
            dstage = arena.alloc([8, S], F32)
            dsb = Buf("dstage")
            for c in range(8):
                sc.op("vector", lambda e, c=c: e.tensor_copy(out=dstage[:, c, :], in_=mixT[:, c, :]),
                      r=[mixb[c][t] for t in range(NT)], w=[dsb])
            sc.dma("sync", dbgv[:, :, :], dstage[:, :, :], dslot, r=[dsb])
            sc.barrier()

        arena.reset()
        hT = arena.alloc([8, S], F32)
        hb = [[Buf(f"h{c}_{t}") for t in range(NT)] for c in range(8)]
        hslot = [sc.new_slot(f"h{t}") for t in range(NT)]
        FG = 8
        wup = [arena.alloc([8, 512], BF16) for _ in range(2)]
        wdn = [arena.alloc([4, 1024], BF16) for _ in range(2)]
        wupb = [Buf("wup0"), Buf("wup1")]
        wdnb = [Buf("wdn0"), Buf("wdn1")]
        wupslot = [sc.new_slot("wup0"), sc.new_slot("wup1")]
        wdnslot = [sc.new_slot("wdn0"), sc.new_slot("wdn1")]
        uT = [arena.alloc([4, TW], BF16) for _ in range(2)]
        ub = [[Buf(f"u{i}_{f}") for f in range(4)] for i in range(2)]
        rr = [arena.alloc([1, TW], F32)[:, 0, :] for _ in range(2)]
        rrb = [Buf("rr0"), Buf("rr1")]

        for t in range(NT):
            cols = slice(t * TW, (t + 1) * TW)
            sc.dma("sync", hT[:, :, cols], xTv[:, :, cols], hslot[t], w=[hb[c][t] for c in range(8)])

        def load_ffn_group(G):
            k = G % 2
            sc.dma("gpsimd", wup[k].rearrange("p a b -> p (a b)"), wup_d[G, :, :], wupslot[k], w=[wupb[k]])
            sc.dma("gpsimd", wdn[k].rearrange("p a b -> p (a b)"), wdn_d[G, :, :], wdnslot[k], w=[wdnb[k]])

        load_ffn_group(0)
        load_ffn_group(1)

        brot = Rot([0, 1, 2, 3, 4, 5, 6])
        for t in range(NT):
            cols = slice(t * TW, (t + 1) * TW)
            for dc in range(8):
                bi = brot.next()
                for ec in range(8):
                    sc.op("tensor", lambda e, bi=bi, ec=ec, dc=dc, cols=cols: e.matmul(
                        banks[bi][:, :], lhsT=wo[:, ec, dc * 128:(dc + 1) * 128], rhs=mixT[:, ec, cols],
                        start=(ec == 0), stop=(ec == 7)), r=[wob, mixb[ec][t]], w=[bankb[bi]])
                sc.op("vector", lambda e, bi=bi, dc=dc, cols=cols: e.tensor_tensor(
                    out=hT[:, dc, cols], in0=banks[bi][:, :], in1=hT[:, dc, cols], op=ALU.add),
                    r=[bankb[bi]], w=[hb[dc][t]])

        def dump_direct(src3d, bufs):
            dslot2 = sc.new_slot("dbg2")
            dbgv2 = dbg_d.rearrange("(c p) t -> p c t", p=128)
            sc.dma("sync", dbgv2[:, :, :], src3d[:, :, :], dslot2, r=bufs)
            sc.barrier()
            block = es.enter_context(nc.Block())
            sc.emit(block)
            return nc

        if stop == "C":
            return dump_direct(hT, [b for l in hb for b in l])

        sc.fence(("vector",), [wob])
        for t in range(NT):
            cols = slice(t * TW, (t + 1) * TW)
            rs, rb = rms_stats(lambda c, cols=cols: hT[:, c, cols], [hb[c][t] for c in range(8)], 8, 7, 1.0 / D, ones[:, :])
            for c in range(8):
                sc.op("vector", lambda e, c=c, rs=rs, cols=cols: e.scalar_tensor_tensor(
                    out=nT[:, c, cols], in0=hT[:, c, cols], scalar=g2(c), in1=rs[:, :], op0=ALU.mult, op1=ALU.mult),
                    r=[hb[c][t], rb, b_gv], w=[nTb[t][c]])

        sc.fence(("vector",), [b for l in mixb for b in l])
        ost = [mixT[:, 2 * i:2 * i + 2, :].rearrange("p a b -> p (a b)").bitcast(F32).rearrange("p (a b) -> p a b", a=4)
               for i in range(4)]
        ostb = [[Buf(f"ost{i}_{j}") for j in range(2)] for i in range(4)]
        oslot = [[sc.new_slot(f"o{i}_{j}") for j in range(2)] for i in range(4)]
        orot = Rot([0, 1, 2, 3])
        def final_tile(t):
            cols = slice(t * TW, (t + 1) * TW)
            rs, rb = rms_stats(lambda c, cols=cols: hT[:, c, cols], [hb[c][t] for c in range(8)], 8, 7, 1.0 / D, ones[:, :])
            for half in range(2):
                oi = orot.next()
                for cc in range(4):
                    c = half * 4 + cc
                    q = cc // 2
                    sc.op("vector", lambda e, c=c, cc=cc, oi=oi, rs=rs, cols=cols: e.scalar_tensor_tensor(
                        out=ost[oi][:, cc, :], in0=hT[:, c, cols], scalar=gf(c), in1=rs[:, :], op0=ALU.mult, op1=ALU.mult),
                        r=[hb[c][t], rb, b_gv], w=[ostb[oi][q]])
                    if cc % 2 == 1:
                        sc.dma("sync", outTv[:, half * 4 + 2 * q:half * 4 + 2 * q + 2, cols], ost[oi][:, 2 * q:2 * q + 2, :],
                               oslot[oi][q], r=[ostb[oi][q]])

        urot = Rot([0, 1])
        rrot = Rot([0, 1])
        def ffn_up(G, t, ui):
            k = G % 2
            cols = slice(t * TW, (t + 1) * TW)
            for fc in range(4):
                bi = brot.next()
                for dc in range(8):
                    sc.op("tensor", lambda e, bi=bi, fc=fc, dc=dc: e.matmul(
                        banks[bi][:, :], lhsT=wup[k][:, dc, fc * 128:(fc + 1) * 128], rhs=nT[:, dc, cols],
                        start=(dc == 0), stop=(dc == 7)), r=[wupb[k], nTb[t][dc]], w=[bankb[bi]])
                ri = rrot.next()
                sc.op("scalar", lambda e, bi=bi, ri=ri: e.activation(out=rr[ri], in_=banks[bi][:, :], func=AF.Relu),
                      r=[bankb[bi]], w=[rrb[ri]])
                sc.op("vector", lambda e, ri=ri, fc=fc: e.tensor_tensor(
                    out=uT[ui][:, fc, :], in0=rr[ri], in1=rr[ri], op=ALU.mult), r=[rrb[ri]], w=[ub[ui][fc]])

        def ffn_down(G, t, ui):
            k = G % 2
            cols = slice(t * TW, (t + 1) * TW)
            for dc in range(8):
                bi = brot.next()
                for fc in range(4):
                    sc.op("tensor", lambda e, bi=bi, fc=fc, dc=dc: e.matmul(
                        banks[bi][:, :], lhsT=wdn[k][:, fc, dc * 128:(dc + 1) * 128], rhs=uT[ui][:, fc, :],
                        start=(fc == 0), stop=(fc == 3)), r=[wdnb[k], ub[ui][fc]], w=[bankb[bi]])
                sc.op("vector", lambda e, bi=bi, dc=dc: e.tensor_tensor(
                    out=hT[:, dc, cols], in0=banks[bi][:, :], in1=hT[:, dc, cols], op=ALU.add),
                    r=[bankb[bi]], w=[hb[dc][t]])

        seq = [(G, t) for G in range(FG) for t in range(NT)]
        uis = [i % 2 for i in range(len(seq))]
        for i in range(len(seq) + 1):
            if i < len(seq):
                ffn_up(seq[i][0], seq[i][1], uis[i])
            if i >= 1:
                G, t = seq[i - 1]
                ffn_down(G, t, uis[i - 1])
                if G == FG - 1 and t >= 1:
                    final_tile(t - 1)
                if t == NT - 1 and G + 2 < FG:
                    load_ffn_group(G + 2)

        final_tile(NT - 1)

        sc.barrier()

        block = es.enter_context(nc.Block())
        sc.emit(block)
    return nc


def _const_tables():
    ct = np.zeros((128, NCT), np.float32)
    ct[:, 0:128] = np.eye(128, dtype=np.float32)
    j = np.arange(128)[:, None]
    s = np.arange(128)[None, :]
    ct[:, 128:256] = np.where(j >= s, -1.0, 0.0)
    ss_ = np.arange(128)[:, None]
    tt = np.arange(128)[None, :]
    ct[:, 256:384] = np.where(ss_ < tt, 0.0, NEG)
    for h in range(4):
        sl = SLOPES[h]
        d = np.where(ss_ <= tt, 0.0, np.where((ss_ // 64) <= (tt // 64), -2.0 * sl * (ss_ - tt), NEG))
        ct[:, 384 + 128 * h:512 + 128 * h] = d
    for Jp in range(16):
        ct[:, 896 + Jp * 128 + Jp] = 1.0
    nu = np.zeros((128, NB, 128), np.float32)
    for Jp in range(16):
        for J in range(16):
            if Jp > J:
                nu[Jp, J, :] = -1.0
    t = np.arange(S)
    qaug = np.stack([-(t % 128), -128.0 * (t // 128), np.ones(S), np.ones(S)]).astype(np.float32)
    kaug = np.zeros((4, 4, S), np.float32)
    for h in range(4):
        sl = SLOPES[h]
        kaug[h] = np.stack([sl * np.ones(S), sl * np.ones(S), sl * (t % 128), sl * 128.0 * (t // 128)])
    return ct, nu.reshape(128, NB * 128), qaug, kaug


def _prep_shared(norm1_g, w_in, lambda_q1, lambda_k1, lambda_q2, lambda_k2, diff_subln_g, sb_norm_g,
                 w_out, norm2_g, w_up, w_down, final_norm_g):
    f = np.float32
    w_in = np.asarray(w_in, f)[0]
    groups = []
    for g in range(8):
        if g < 4:
            qc, kc, vc = g * 128, 512 + g * 128, 1024 + g * 128
        else:
            p = g - 4
            qc, kc, vc = 1536 + p * 128, 2048 + p * 128, 2560 + p * 128
        wgm = np.concatenate([w_in[:, qc:qc + 128], w_in[:, kc:kc + 128], w_in[:, vc:vc + 128]], axis=1)
        groups.append(wgm.reshape(8, 128, 384).transpose(1, 0, 2).reshape(128, 8 * 384))
    win = np.ascontiguousarray(np.stack(groups))
    wout = np.ascontiguousarray(np.asarray(w_out, f)[0].reshape(8, 128, 1024).transpose(1, 0, 2).reshape(128, 8 * 1024))
    wu = np.asarray(w_up, f)[0]
    wup = np.ascontiguousarray(np.stack([
        wu[:, G * 512:(G + 1) * 512].reshape(8, 128, 512).transpose(1, 0, 2).reshape(128, 8 * 512) for G in range(8)]))
    wd = np.asarray(w_down, f)[0]
    wdn = np.ascontiguousarray(np.stack([
        wd[G * 512:(G + 1) * 512, :].reshape(4, 128, 1024).transpose(1, 0, 2).reshape(128, 4 * 1024) for G in range(8)]))
    gv = np.zeros((128, 26), f)
    gv[:, 0:8] = np.asarray(norm1_g, f)[0].reshape(8, 128).T
    gv[:, 8:16] = np.asarray(norm2_g, f)[0].reshape(8, 128).T
    gv[:, 16:24] = np.asarray(final_norm_g, f).reshape(8, 128).T
    gv[:, 24] = np.asarray(diff_subln_g, f)[0]
    gv[:, 25] = np.concatenate([np.asarray(sb_norm_g, f)[0]] * 2)
    lamv = np.concatenate([np.asarray(a, f)[0] for a in (lambda_q1, lambda_k1, lambda_q2, lambda_k2)])
    lamv = np.ascontiguousarray(np.broadcast_to(lamv[None, :], (128, 256)))
    ct, nu, qaug, kaug = _const_tables()
    return {"win": win, "wout": wout, "wup": wup, "wdn": wdn, "gv": gv, "lamv": lamv,
            "ctab": ct, "nutab": nu, "qaug": qaug, "kaug": kaug}


_CACHE = {}


def kernel(x, norm1_g, w_in, lambda_q1, lambda_k1, lambda_q2, lambda_k2, diff_subln_g, sb_norm_g,
           w_out, norm2_g, w_up, w_down, final_norm_g):
    debug = bool(int(os.environ.get("MK_DEBUG", "0")))
    x = np.asarray(x, np.float32)
    shared = _prep_shared(norm1_g, w_in, lambda_q1, lambda_k1, lambda_q2, lambda_k2, diff_subln_g, sb_norm_g,
                          w_out, norm2_g, w_up, w_down, final_norm_g)
    n = x.shape[0]
    in_maps = []
    for b in range(n):
        m = dict(shared)
        m["xT"] = np.ascontiguousarray(x[b].T)
        in_maps.append(m)
    key = ("nc", debug)
    if key not in _CACHE:
        _CACHE[key] = build_program(debug=debug)
    nc = _CACHE[key]
    res = run_bass_kernel_spmd(nc, in_maps, core_ids=list(range(n)))
    out = np.stack([np.ascontiguousarray(r["outT"].T) for r in res.results]).astype(np.float32)
    if debug:
        kernel.dbg = np.stack([r["dbg"] for r in res.results])
    return out
```
